# Optimizing a Trainium2 kernel written in Bass

```python
import jax, jax.numpy as jnp
from jax import lax
import numpy as np

D_MODEL = 2048
BATCH = 16
SEQ = 2048
DEPTH = 4

N_MIXERS = 2
N_HGRN_LAYERS = (DEPTH + 1) // 2
N_GDN_LAYERS = DEPTH // 2

HG_DK = 128
HG_HEADS = D_MODEL // HG_DK
HG_DV = D_MODEL // HG_HEADS
HG_CHUNK = 16
HG_IN = 2 * HG_HEADS * HG_DK + 2 * HG_HEADS * HG_DV

GDN_DK = 128
GDN_DV = 128
GDN_K_HEADS = D_MODEL // 128
GDN_V_HEADS = 2 * GDN_K_HEADS
GDN_KEY_DIM = GDN_K_HEADS * GDN_DK
GDN_VAL_DIM = GDN_V_HEADS * GDN_DV
GDN_CONV = 4
GDN_CONV_DIM = 2 * GDN_KEY_DIM + GDN_VAL_DIM
GDN_IN = GDN_CONV_DIM + GDN_VAL_DIM + 2 * GDN_V_HEADS
GDN_CHUNK = 64

FFN_HIDDEN = -(-(8 * D_MODEL) // (3 * 256)) * 256
N_MOD = 6
EPS = 1e-6

kernel_name = "hybrid_hgrn2_gdn_adaln_trunk"


def rms_norm(x, w, eps=EPS):
    xf = x.astype(jnp.float32)
    y = xf * lax.rsqrt(jnp.mean(xf * xf, axis=-1, keepdims=True) + eps)
    return (y * w.astype(jnp.float32)).astype(x.dtype)


def l2_normalize(x, eps=EPS):
    xf = x.astype(jnp.float32)
    return xf * lax.rsqrt(jnp.sum(xf * xf, axis=-1, keepdims=True) + eps)


def causal_depthwise_conv(x, w):
    K = w.shape[0]
    S = x.shape[1]
    xp = jnp.pad(x, ((0, 0), (K - 1, 0), (0, 0)))
    out = xp[:, 0:S, :] * w[0]
    for j in range(1, K):
        out = out + xp[:, j:j + S, :] * w[j]
    return out


def hgrn2_chunked(q, k, v, log_f):
    B, S, H, DK = q.shape
    DV = v.shape[-1]
    C = HG_CHUNK
    N = S // C

    def to_chunks(t):
        return t.reshape(B, N, C, H, t.shape[-1]).transpose(1, 0, 3, 2, 4)

    qc, kc, vc, gc = to_chunks(q), to_chunks(k), to_chunks(v), to_chunks(log_f)
    causal = jnp.tril(jnp.ones((C, C), dtype=bool))

    def step(state, inp):
        q_n, k_n, v_n, g_n = inp
        b = jnp.cumsum(g_n, axis=-2)
        b_last = b[..., -1:, :]
        o_inter = jnp.einsum('bhtk,bhkv->bhtv', q_n * jnp.exp(b), state)
        diff = jnp.where(causal[:, :, None], b[..., :, None, :] - b[..., None, :, :], -jnp.inf)
        scores = jnp.einsum('bhtk,bhsk,bhtsk->bhts', q_n, k_n, jnp.exp(diff))
        o = o_inter + jnp.einsum('bhts,bhsv->bhtv', scores, v_n)
        state = state * jnp.exp(b_last[..., 0, :])[..., None] + jnp.einsum(
            'bhsk,bhsv->bhkv', k_n * jnp.exp(b_last - b), v_n)
        return state, o

    state0 = jnp.zeros((B, H, DK, DV), jnp.float32)
    _, oc = lax.scan(step, state0, (qc, kc, vc, gc))
    return oc.transpose(1, 0, 3, 2, 4).reshape(B, S, H, DV)


def hgrn2_mixer(h, w_in, lower_bound, norm_w, w_out):
    B, S, _ = h.shape
    proj = h @ w_in
    hk = HG_HEADS * HG_DK
    hv = HG_HEADS * HG_DV
    q = jax.nn.silu(proj[..., :hk].astype(jnp.float32))
    f_logit = proj[..., hk:2 * hk].astype(jnp.float32)
    i_val = proj[..., 2 * hk:2 * hk + hv].astype(jnp.float32)
    out_gate = proj[..., 2 * hk + hv:].astype(jnp.float32)
    lb = lower_bound.astype(jnp.float32)
    log_f = jnp.logaddexp(jnp.log(lb), jnp.log1p(-lb) + jax.nn.log_sigmoid(f_logit))
    k = -jnp.expm1(log_f)
    o = hgrn2_chunked(q.reshape(B, S, HG_HEADS, HG_DK), k.reshape(B, S, HG_HEADS, HG_DK),
                      i_val.reshape(B, S, HG_HEADS, HG_DV), log_f.reshape(B, S, HG_HEADS, HG_DK))
    o = rms_norm(o, norm_w) * jax.nn.sigmoid(out_gate.reshape(B, S, HG_HEADS, HG_DV))
    return o.reshape(B, S, hv).astype(h.dtype) @ w_out


def gated_delta_chunked(q, k, v, beta, g):
    B, S, H, DK = q.shape
    DV = v.shape[-1]
    C = GDN_CHUNK
    N = S // C

    def chunk_vec(t):
        return t.reshape(B, N, C, H, t.shape[-1]).transpose(0, 3, 1, 2, 4)

    def chunk_sca(t):
        return t.reshape(B, N, C, H).transpose(0, 3, 1, 2)

    qc, kc, vc = chunk_vec(q), chunk_vec(k), chunk_vec(v)
    bc, gcum = chunk_sca(beta), jnp.cumsum(chunk_sca(g), axis=-1)
    causal = jnp.tril(jnp.ones((C, C), dtype=bool))
    strict = jnp.tril(jnp.ones((C, C), dtype=bool), -1)
    decay = jnp.exp(jnp.where(causal, gcum[..., :, None] - gcum[..., None, :], -jnp.inf))
    kk = jnp.einsum('bhntk,bhnsk->bhnts', kc, kc)
    a_low = jnp.where(strict, bc[..., :, None] * kk * decay, 0.0)
    eye = jnp.broadcast_to(jnp.eye(C, dtype=jnp.float32), a_low.shape)
    t_inv = lax.linalg.triangular_solve(a_low, eye, left_side=True, lower=True, unit_diagonal=True)
    u = jnp.einsum('bhnts,bhnsv->bhntv', t_inv, vc * bc[..., None])
    w = jnp.einsum('bhnts,bhnsk->bhntk', t_inv, kc * (bc * jnp.exp(gcum))[..., None])
    attn = jnp.where(causal, jnp.einsum('bhntk,bhnsk->bhnts', qc, kc) * decay, 0.0)

    def step(state, inp):
        q_n, k_n, u_n, w_n, attn_n, g_n = inp
        v_new = u_n - jnp.einsum('bhck,bhkv->bhcv', w_n, state)
        o = jnp.einsum('bhck,bhkv->bhcv', q_n * jnp.exp(g_n)[..., None], state) + jnp.einsum(
            'bhts,bhsv->bhtv', attn_n, v_new)
        g_last = g_n[..., -1:]
        state = state * jnp.exp(g_last)[..., None] + jnp.einsum(
            'bhck,bhcv->bhkv', k_n * jnp.exp(g_last - g_n)[..., None], v_new)
        return state, o

    mv = lambda t: jnp.moveaxis(t, 2, 0)
    state0 = jnp.zeros((B, H, DK, DV), jnp.float32)
    _, oc = lax.scan(step, state0, (mv(qc), mv(kc), mv(u), mv(w), mv(attn), mv(gcum)))
    return jnp.moveaxis(oc, 0, 2).transpose(0, 2, 3, 1, 4).reshape(B, S, H, DV)


def gdn_mixer(h, w_in, conv_w, a_log, dt_bias, norm_w, w_out):
    B, S, _ = h.shape
    proj = h @ w_in
    qkv = jax.nn.silu(causal_depthwise_conv(proj[..., :GDN_CONV_DIM], conv_w)).astype(jnp.float32)
    z = proj[..., GDN_CONV_DIM:GDN_CONV_DIM + GDN_VAL_DIM].astype(jnp.float32)
    b_logit = proj[..., GDN_CONV_DIM + GDN_VAL_DIM:GDN_CONV_DIM + GDN_VAL_DIM + GDN_V_HEADS].astype(jnp.float32)
    a_in = proj[..., GDN_CONV_DIM + GDN_VAL_DIM + GDN_V_HEADS:].astype(jnp.float32)
    rep = GDN_V_HEADS // GDN_K_HEADS
    q = l2_normalize(qkv[..., :GDN_KEY_DIM].reshape(B, S, GDN_K_HEADS, GDN_DK))
    k = l2_normalize(qkv[..., GDN_KEY_DIM:2 * GDN_KEY_DIM].reshape(B, S, GDN_K_HEADS, GDN_DK))
    v = qkv[..., 2 * GDN_KEY_DIM:].reshape(B, S, GDN_V_HEADS, GDN_DV)
    q = jnp.repeat(q, rep, axis=2) * (GDN_DK ** -0.5)
    k = jnp.repeat(k, rep, axis=2)
    beta = jax.nn.sigmoid(b_logit)
    g = -jnp.exp(a_log.astype(jnp.float32)) * jax.nn.softplus(a_in + dt_bias.astype(jnp.float32))
    o = gated_delta_chunked(q, k, v, beta, g)
    o = rms_norm(o, norm_w) * jax.nn.silu(z.reshape(B, S, GDN_V_HEADS, GDN_DV))
    return o.reshape(B, S, GDN_VAL_DIM).astype(h.dtype) @ w_out


def swiglu_ffn(h, w_gate_up, w_down):
    gu = h @ w_gate_up
    return (jax.nn.silu(gu[..., :FFN_HIDDEN]) * gu[..., FFN_HIDDEN:]) @ w_down


def setup_inputs(seed: int = 0) -> dict:
    key = jax.random.key(seed)
    ks = jax.random.split(key, 20)
    f32 = jnp.float32

    def nrm(k, shape, scale):
        return jax.random.normal(k, shape, f32) * scale

    dt = jnp.exp(jax.random.uniform(ks[13], (N_GDN_LAYERS, GDN_V_HEADS), f32,
                                    np.log(1e-3).astype(np.float32), np.log(1e-1).astype(np.float32)))
    return {
        "x": nrm(ks[0], (BATCH, SEQ, D_MODEL), 1.0),
        "c": nrm(ks[1], (BATCH, D_MODEL), 1.0),
        "ada_w": nrm(ks[2], (DEPTH, D_MODEL, N_MOD * D_MODEL), 0.5 * D_MODEL ** -0.5),
        "ada_b": nrm(ks[3], (DEPTH, N_MOD * D_MODEL), 0.01),
        "norm_w": 1.0 + nrm(ks[4], (DEPTH, 4, D_MODEL), 0.02),
        "hg_w_in": nrm(ks[5], (N_HGRN_LAYERS, D_MODEL, HG_IN), D_MODEL ** -0.5),
        "hg_lb_logits": nrm(ks[6], (N_HGRN_LAYERS, HG_HEADS * HG_DK), 0.5),
        "hg_norm_w": 1.0 + nrm(ks[7], (N_HGRN_LAYERS, HG_DV), 0.02),
        "hg_w_out": nrm(ks[8], (N_HGRN_LAYERS, HG_HEADS * HG_DV, D_MODEL), (HG_HEADS * HG_DV) ** -0.5),
        "gdn_w_in": nrm(ks[9], (N_GDN_LAYERS, D_MODEL, GDN_IN), D_MODEL ** -0.5),
        "gdn_conv_w": nrm(ks[10], (N_GDN_LAYERS, GDN_CONV, GDN_CONV_DIM), GDN_CONV ** -0.5),
        "gdn_A_log": jnp.log(jax.random.uniform(ks[11], (N_GDN_LAYERS, GDN_V_HEADS), f32, 1.0, 16.0)),
        "gdn_dt_bias": dt + jnp.log(-jnp.expm1(-dt)),
        "gdn_norm_w": 1.0 + nrm(ks[12], (N_GDN_LAYERS, GDN_DV), 0.02),
        "gdn_w_out": nrm(ks[14], (N_GDN_LAYERS, GDN_VAL_DIM, D_MODEL), GDN_VAL_DIM ** -0.5),
        "ffn_w_gate_up": nrm(ks[15], (DEPTH, D_MODEL, 2 * FFN_HIDDEN), D_MODEL ** -0.5),
        "ffn_w_down": nrm(ks[16], (DEPTH, FFN_HIDDEN, D_MODEL), FFN_HIDDEN ** -0.5),
    }


def reference(x, c, ada_w, ada_b, norm_w, hg_w_in, hg_lb_logits, hg_norm_w, hg_w_out,
              gdn_w_in, gdn_conv_w, gdn_A_log, gdn_dt_bias, gdn_norm_w, gdn_w_out,
              ffn_w_gate_up, ffn_w_down):
    lb_table = jnp.cumsum(jax.nn.softmax(hg_lb_logits.astype(jnp.float32), axis=0), axis=0)
    lb_table = lb_table - lb_table[:1]
    c_act = jax.nn.silu(c)
    for layer in range(DEPTH):
        mod = c_act @ ada_w[layer] + ada_b[layer]
        shift_m, scale_m, gate_m, shift_f, scale_f, gate_f = jnp.split(mod[:, None, :], N_MOD, axis=-1)
        h = rms_norm(x, norm_w[layer, 0]) * (1.0 + scale_m) + shift_m
        j = layer // N_MIXERS
        if layer % N_MIXERS == 0:
            y = hgrn2_mixer(h, hg_w_in[j], lb_table[j], hg_norm_w[j], hg_w_out[j])
        else:
            y = gdn_mixer(h, gdn_w_in[j], gdn_conv_w[j], gdn_A_log[j], gdn_dt_bias[j],
                          gdn_norm_w[j], gdn_w_out[j])
        x = x + gate_m * rms_norm(y, norm_w[layer, 1])
        h = rms_norm(x, norm_w[layer, 2]) * (1.0 + scale_f) + shift_f
        y = swiglu_ffn(h, ffn_w_gate_up[layer], ffn_w_down[layer])
        x = x + gate_f * rms_norm(y, norm_w[layer, 3])
    return x
```

```python
import contextlib
import numpy as np
import concourse.bass as bass
import concourse.mybir as mybir
from concourse.bass_utils import run_bass_kernel_spmd

F32 = mybir.dt.float32
BF16 = mybir.dt.bfloat16
AF = mybir.ActivationFunctionType
ALU = mybir.AluOpType

SAME_ENG_SYNC = True
SEM_BLK = 30000
SEM_BLK_DMA = 30000

D = 2048
KC = 16
T = 512
NBLK = 4
FC = 44
FFN_H = 5632
EPS = 1e-6
N_CORES = 8
SEQ = 2048
DEPTH = 4
CST_COLS = 1920
GDN_NLEV = 5
GDN_INV_F32 = True


class Buf:
    __slots__ = ("name", "lw", "readers")

    def __init__(self, name):
        self.name = name
        self.lw = None
        self.readers = {}


class _Op:
    __slots__ = ("waits", "fn", "needs_inc", "dma")

    def __init__(self, fn):
        self.waits = []
        self.fn = fn
        self.needs_inc = False
        self.dma = None


class Sched:
    def __init__(self, nc, stack):
        self.nc = nc
        self.stack = stack
        self.eng = {"pe": nc.tensor, "act": nc.scalar, "dve": nc.vector, "pool": nc.gpsimd, "sp": nc.sync}
        self.ops = {k: [] for k in self.eng}
        self.seen = {k: {} for k in self.eng}
        self.sem = {}
        self.semtot = {}
        self.dgen = {}

    def dsem(self, base):
        g = self.dgen.get(base, 0)
        name = "%s_g%d" % (base, g)
        if name in self.semtot and self.semtot[name] + 16 > SEM_BLK_DMA:
            g += 1
            name = "%s_g%d" % (base, g)
        self.dgen[base] = g
        if name not in self.sem:
            self.sem[name] = self.stack.enter_context(self.nc.semaphore(name))
            self.semtot[name] = 0
        return name

    def _need(self, e, op, ev):
        if ev is None:
            return
        if ev[0] == "e":
            f, idx = ev[1], ev[2]
            if f == e:
                if e in ("pe", "sp") or not SAME_ENG_SYNC:
                    return
            key = "e_" + f
            if self.seen[e].get(key, -1) >= idx:
                return
            self.seen[e][key] = idx
            self.ops[f][idx].needs_inc = True
            op.waits.append(ev)
        else:
            name = ev[1]
            tot = self.semtot[name]
            if self.seen[e].get(name, -1) >= tot:
                return
            self.seen[e][name] = tot
            op.waits.append(("d", name, tot))

    def op(self, e, fn, r=(), w=()):
        op = _Op(fn)
        idx = len(self.ops[e])
        for b in r:
            self._need(e, op, b.lw)
        for b in w:
            self._need(e, op, b.lw)
            for ev in b.readers.values():
                self._need(e, op, ev)
        self.ops[e].append(op)
        ev = ("e", e, idx)
        for b in w:
            b.lw = ev
            b.readers = {}
        for b in r:
            if b.lw is not ev:
                b.readers[e] = ev
        return ev

    def dma(self, q, out, in_, r=(), w=(), sem=None, **kw):
        name = self.dsem(sem)
        op = _Op(lambda eng: eng.dma_start(out=out, in_=in_, **kw))
        if self.semtot[name] > 0:
            self._need(q, op, ("d", name, self.semtot[name]))
        for b in r:
            self._need(q, op, b.lw)
        for b in w:
            self._need(q, op, b.lw)
            for ev in b.readers.values():
                self._need(q, op, ev)
        self.semtot[name] += 16
        op.dma = name
        self.ops[q].append(op)
        ev = ("d", name, self.semtot[name])
        for b in w:
            b.lw = ev
            b.readers = {}
        for b in r:
            b.readers["d_" + name] = ev
        return ev

    def wait_all(self, e, bufs):
        op = _Op(None)
        for b in bufs:
            self._need(e, op, b.lw)
            for ev in b.readers.values():
                self._need(e, op, ev)
        self.ops[e].append(op)

    def emit(self):
        nc = self.nc
        cum = {}
        self.maxcount = {}
        for e, lst in self.ops.items():
            c = 0
            arr = []
            for o in lst:
                if o.needs_inc:
                    c += 1
                arr.append(((c - 1) // SEM_BLK if c > 0 else 0, (c - 1) % SEM_BLK + 1 if c > 0 else 0))
            cum[e] = arr
            self.maxcount[e] = c
            for j in range((c + SEM_BLK - 1) // SEM_BLK + 1):
                nm = "e_%s_%d" % (e, j)
                self.sem[nm] = self.stack.enter_context(nc.semaphore(nm))
        sem = self.sem

        def run(e, handle):
            for i, o in enumerate(self.ops[e]):
                for ev in o.waits:
                    if ev[0] == "e":
                        blk, c = cum[ev[1]][ev[2]]
                        handle.wait_ge(sem["e_%s_%d" % (ev[1], blk)], c)
                    else:
                        handle.wait_ge(sem[ev[1]], ev[2])
                if o.fn is None:
                    continue
                ins = o.fn(handle)
                if o.dma is not None:
                    ins.then_inc(sem[o.dma], 16)
                elif o.needs_inc:
                    ins.then_inc(sem["e_%s_%d" % (e, cum[e][i][0])], 1)

        with nc.Block() as block:
            @block.sync
            def _(h):
                run("sp", h)

            @block.scalar
            def _(h):
                run("act", h)

            @block.vector
            def _(h):
                run("dve", h)

            @block.gpsimd
            def _(h):
                run("pool", h)

            @block.tensor
            def _(h):
                run("pe", h)


class Prog:
    def __init__(self, S, NSEQ, layer_ids, dbg=False, stop_after=None):
        self.stop_after = stop_after
        self.S = S
        self.NSEQ = NSEQ
        self.layer_ids = list(layer_ids)
        self.NT = S // T
        self.dbg = dbg
        self.nc = bass.Bass("TRN2", target_bir_lowering=False)
        with contextlib.ExitStack() as st:
            self.st = st
            self.sc = Sched(self.nc, st)
            self.declare()
            self.consts()
            self.body()
            self.sc.emit()

    def sb(self, name, shape, dt):
        return self.st.enter_context(self.nc.sbuf_tensor(name, shape, dt))

    def dram_in(self, name, shape, dt=F32):
        return self.nc.dram_tensor(name, list(shape), dt, kind="ExternalInput").ap()

    def dram_tmp(self, name, shape, dt):
        return self.nc.dram_tensor(name, list(shape), dt, kind="Internal").ap()

    def ACT(self, out, in_, func, r, w, **kw):
        self.sc.op("act", lambda e: e.activation(out=out, in_=in_, func=func, **kw), r=r, w=w)

    def MM(self, out, lhsT, rhs, start, stop, r, w):
        self.sc.op("pe", lambda e: e.matmul(out, lhsT=lhsT, rhs=rhs, start=start, stop=stop), r=r, w=w)

    def TR(self, out, in_, ident, r, w):
        self.sc.op("pe", lambda e: e.transpose(out, in_, ident), r=r, w=w)

    def TT(self, eng, out, in0, in1, op, r, w):
        self.sc.op(eng, lambda e: e.tensor_tensor(out=out, in0=in0, in1=in1, op=op), r=r, w=w)

    def TS(self, eng, out, in0, s1, s2, op0, op1, r, w):
        if s2 is None:
            self.sc.op(eng, lambda e: e.tensor_scalar(out=out, in0=in0, scalar1=s1, scalar2=None, op0=op0), r=r, w=w)
        else:
            self.sc.op(eng, lambda e: e.tensor_scalar(out=out, in0=in0, scalar1=s1, scalar2=s2, op0=op0, op1=op1), r=r, w=w)

    def STT(self, out, in0, scalar, in1, op0, op1, r, w, accum_out=None):
        if accum_out is None:
            self.sc.op("dve", lambda e: e.scalar_tensor_tensor(out=out, in0=in0, scalar=scalar, in1=in1, op0=op0, op1=op1), r=r, w=w)
        else:
            self.sc.op("dve", lambda e: e.scalar_tensor_tensor(out=out, in0=in0, scalar=scalar, in1=in1, op0=op0, op1=op1, accum_out=accum_out), r=r, w=w)

    def CP(self, eng, out, in_, r, w):
        if eng == "act":
            self.sc.op("act", lambda e: e.copy(out=out, in_=in_), r=r, w=w)
        else:
            self.sc.op(eng, lambda e: e.tensor_copy(out=out, in_=in_), r=r, w=w)

    def MEMSET(self, eng, ap, val, w):
        self.sc.op(eng, lambda e: e.memset(ap, val), w=w)

    def DMA(self, q, out, in_, r, w, sem, **kw):
        self.sc.dma(q, out, in_, r=r, w=w, sem=sem, **kw)

    def declare(self):
        S, NSEQ = self.S, self.NSEQ
        NTOK = S * NSEQ
        self.x_in = self.dram_in("x", [NTOK, D])
        self.cT = self.dram_in("cT", [128, KC, NSEQ])
        self.out = self.nc.dram_tensor("out", [NTOK, D], F32, kind="ExternalOutput").ap()
        self.B_stream = [[Buf("xs_%d_%d" % (s, t)) for t in range(self.NT)] for s in range(NSEQ)]
        self.W = {}
        for li in self.layer_ids:
            w = {}
            w["ada_w"] = self.dram_in("ada_w_%d" % li, [D, 6 * D])
            w["ada_bT"] = self.dram_in("ada_bT_%d" % li, [128, 6 * KC])
            w["normT"] = self.dram_in("normT_%d" % li, [128, 4 * KC])
            w["ffn_gu"] = self.dram_in("ffn_gu_%d" % li, [2 * FC, 128, KC * 128])
            w["ffn_dn"] = self.dram_in("ffn_dn_%d" % li, [4, 128, FC * 512])
            w["ffn_gu_b"] = self.dram_tmp("ffn_gu_b_%d" % li, [2 * FC, 128, KC * 128], BF16)
            w["ffn_dn_b"] = self.dram_tmp("ffn_dn_b_%d" % li, [4, 128, FC * 512], BF16)
            if li % 2 == 0:
                w["in"] = self.dram_in("hg_in_%d" % li, [64, 128, KC * 128])
                w["outw"] = self.dram_in("hg_out_%d" % li, [4, 128, 16 * 512])
                w["in_b"] = self.dram_tmp("hg_in_b_%d" % li, [64, 128, KC * 128], BF16)
                w["out_b"] = self.dram_tmp("hg_out_b_%d" % li, [4, 128, 16 * 512], BF16)
                w["lbT"] = self.dram_in("hg_lbT_%d" % li, [128, 2 * 16])
                w["hnw"] = self.dram_in("hg_nw_%d" % li, [128, 1])
            else:
                w["in"] = self.dram_in("gdn_in_%d" % li, [96, 128, KC * 128])
                w["ba"] = self.dram_in("gdn_ba_%d" % li, [128, KC * 64])
                w["outw"] = self.dram_in("gdn_out_%d" % li, [4, 128, 32 * 512])
                w["in_b"] = self.dram_tmp("gdn_in_b_%d" % li, [96, 128, KC * 128], BF16)
                w["ba_b"] = self.dram_tmp("gdn_ba_b_%d" % li, [128, KC * 64], BF16)
                w["out_b"] = self.dram_tmp("gdn_out_b_%d" % li, [4, 128, 32 * 512], BF16)
                w["convT"] = self.dram_in("gdn_convT_%d" % li, [128, 64 * 4])
                w["aldt"] = self.dram_in("gdn_aldt_%d" % li, [128, 64])
                w["gnw"] = self.dram_in("gdn_nw_%d" % li, [128, 1])
            w["B_cast"] = Buf("cast_%d" % li)
            self.W[li] = w
        nl = len(self.layer_ids)
        self.gvec = self.dram_tmp("gvec", [nl * 2 * NSEQ, D], F32)
        self.B_gvec = Buf("gvec")

        self.cst_in = self.dram_in("cst", [128, CST_COLS])
        self.ident_b = self.sb("ident_b", [128, 128], BF16)
        self.ones_b = self.sb("ones_b", [128, 128], BF16)
        self.ones1_b = self.sb("ones1_b", [128, 128], BF16)
        self.B_const = Buf("const")
        self.xblk = [self.sb("xblk%d" % i, [128, D], F32) for i in range(2)]
        self.B_xblk = [Buf("xblk%d" % i) for i in range(2)]
        self.gbc = self.sb("gbc", [128, D], F32)
        self.B_gbc = Buf("gbc")
        self.hT = self.sb("hT", [128, KC, T], BF16)
        self.B_hT = Buf("hT")
        self.hid = self.sb("hid", [128, FC, T], BF16)
        self.B_hid = [Buf("hid%d" % i) for i in range(FC)]
        self.state = self.sb("state", [128, 32, 128], F32)
        self.B_state = [Buf("state%d" % i) for i in range(32)]
        self.NU = 4
        self.uring = [self.sb("uring%d" % i, [128, KC, 128], BF16) for i in range(self.NU)]
        self.B_uring = [Buf("uring%d" % i) for i in range(self.NU)]
        self.u_next = 0
        self.pring = [self.sb("pring%d" % i, [128, 8, 512], BF16) for i in range(2)]
        self.B_pring = [Buf("pring%d" % i) for i in range(2)]
        self.p_next = 0
        self.NW = 18
        self.work = self.sb("work", [128, self.NW, 512], F32)
        self.B_work = [Buf("work%d" % i) for i in range(self.NW)]
        base = self.hid[:, 0:8, :].rearrange("p a b -> p (a b)").bitcast(F32)
        n1 = nl * 6 * KC * NSEQ
        n2 = nl * 6 * KC
        n3 = nl * 4 * KC
        self.modT = base[:, 0:n1].rearrange("p (l v k s) -> p l v k s", v=6, k=KC, s=NSEQ)
        self.adab = base[:, n1:n1 + n2].rearrange("p (l v k) -> p l v k", v=6, k=KC)
        self.normT = base[:, n1 + n2:n1 + n2 + n3].rearrange("p (l v k) -> p l v k", v=4, k=KC)
        self.B_modT = Buf("modT")
        self.AB = self.sb("AB", [128, nl, NSEQ, 4, KC], F32)
        self.B_AB = Buf("AB")
        self.cact = self.sb("cact", [128, KC, NSEQ], F32)
        self.B_cact = Buf("cact")
        self.small = self.sb("small", [128, 64], F32)
        self.B_small = [Buf("small%d" % i) for i in range(16)]
        self.B_par = Buf("par")
        self.ps = [self.st.enter_context(self.nc.psum_tensor("ps%d" % i, [128, 512], F32)) for i in range(8)]
        self.B_ps = [Buf("ps%d" % i) for i in range(8)]
        self.rr = 0

    def wslot(self, i, dt=F32):
        ap = self.work[:, i, :]
        return ap if dt == F32 else ap.bitcast(BF16)

    def evac_eng(self):
        self.rr ^= 1
        return "act" if self.rr else "dve"

    def consts(self):
        Bc = self.B_const
        self.cst = self.sb("cst_sb", [128, CST_COLS], F32)
        self.DMA("sp", self.cst[:], self.cst_in, [], [Bc], "ld_small")
        self.ident_f = self.cst[:, 0:128]
        self.CP("dve", self.ident_b[:], self.ident_f, [Bc], [Bc])
        self.MEMSET("pool", self.ones_b[:], 1.0 / 128.0, [Bc])
        self.MEMSET("pool", self.ones1_b[:], 1.0, [Bc])
        self.hg_mask = self.cst[:, 128:256]
        self.hg_reset = self.cst[:, 256:768]
        self.hg_vmask = self.cst[:, 768:1280].rearrange("p (n c) -> p n c", c=128)
        self.B_hgc = Bc

    def cast_layer(self, li):
        w = self.W[li]
        B = w["B_cast"]
        pairs = [("in", "in_b", 8), ("outw", "out_b", 2), ("ffn_gu", "ffn_gu_b", 8), ("ffn_dn", "ffn_dn_b", 4)]
        if li % 2 == 1:
            pairs.append(("ba", "ba_b", 1))
        i = 0
        for src, dst, nsplit in pairs:
            s_ap, d_ap = w[src], w[dst]
            n0 = s_ap.shape[0]
            if nsplit == 1:
                self.DMA("pool", d_ap, s_ap, [], [B], "cast%d_%d" % (li % 2, i))
                i += 1
                continue
            step = (n0 + nsplit - 1) // nsplit
            for a in range(0, n0, step):
                b = min(n0, a + step)
                self.DMA("pool", d_ap[a:b], s_ap[a:b], [], [B], "cast%d_%d" % (li % 2, i))
                i += 1

    def adaln(self):
        NSEQ = self.NSEQ
        nl = len(self.layer_ids)
        Bp = self.B_par
        self.DMA("sp", self.cact[:], self.cT, [], [self.B_cact], "ld_small")
        for k, li in enumerate(self.layer_ids):
            w = self.W[li]
            self.DMA("sp", self.normT[:, k].rearrange("p a b -> p (a b)"), w["normT"], [], [Bp], "ld_small")
            self.DMA("sp", self.adab[:, k].rearrange("p a b -> p (a b)"), w["ada_bT"], [], [Bp], "ld_small")
        self.ACT(self.cact[:], self.cact[:], AF.Silu, [self.B_cact], [self.B_cact])
        NP = 256
        panels = []
        for k, li in enumerate(self.layer_ids):
            for v in range(6):
                for c0 in range(0, D, NP):
                    panels.append((k, li, v, c0))
        pan_t = [self.work[:, 0:8, :].rearrange("p a b -> p (a b)").rearrange("p (k n) -> p k n", n=NP),
                 self.work[:, 8:16, :].rearrange("p a b -> p (a b)").rearrange("p (k n) -> p k n", n=NP)]
        pan_B = [self.B_work[0:8], self.B_work[8:16]]
        bank = self.ps[0]
        for i, (k, li, v, c0) in enumerate(panels):
            sl = i % 2
            src = self.W[li]["ada_w"][:, v * D + c0: v * D + c0 + NP].rearrange("(kc p) n -> p kc n", p=128)
            self.DMA("sp", pan_t[sl], src, [], pan_B[sl], "ld_ada%d" % sl)
            for j in range(NP // 128):
                col = (c0 // 128 + j)
                o = bank[:, col * NSEQ:(col + 1) * NSEQ]
                for kc in range(KC):
                    self.MM(o, pan_t[sl][:, kc, j * 128:(j + 1) * 128], self.cact[:, kc, :], kc == 0, kc == KC - 1,
                            pan_B[sl] + [self.B_cact], [self.B_ps[0]])
            if c0 + NP == D:
                self.TT("dve", self.modT[:, k, v], bank[:, 0:KC * NSEQ].rearrange("p (a s) -> p a s", s=NSEQ),
                        self.adab[:, k, v].unsqueeze(2).to_broadcast([128, KC, NSEQ]), ALU.add,
                        [Bp], [self.B_ps[0], self.B_modT])
        for k, li in enumerate(self.layer_ids):
            for s in range(NSEQ):
                for (dst, nrm, v_scale, v_shift) in ((0, 0, 1, 0), (2, 2, 4, 3)):
                    self.STT(self.AB[:, k, s, dst], self.modT[:, k, v_scale, :, s], 1.0, self.normT[:, k, nrm],
                             ALU.add, ALU.mult, [self.B_modT, Bp], [self.B_AB])
                    self.CP("dve", self.AB[:, k, s, dst + 1], self.modT[:, k, v_shift, :, s], [self.B_modT], [self.B_AB])
                for sub, (v_gate, nrm) in enumerate(((2, 1), (5, 3))):
                    g = self.small[:, 0:KC]
                    self.TT("dve", g, self.modT[:, k, v_gate, :, s], self.normT[:, k, nrm], ALU.mult,
                            [self.B_modT, Bp], [self.B_small[0]])
                    self.MM(self.ps[1][0:KC, 0:128], g, self.ident_f, True, True,
                            [self.B_small[0], self.B_const], [self.B_ps[1]])
                    stg = self.work[0:KC, 16, 0:128]
                    self.CP("dve", stg, self.ps[1][0:KC, 0:128], [], [self.B_ps[1], self.B_work[16]])
                    row = (k * 2 + sub) * NSEQ + s
                    self.DMA("pool", self.gvec[row].rearrange("(kc p) -> kc p", p=128), stg,
                             [self.B_work[16]], [self.B_gvec], "st_small")

    def load_unit(self, dram_units, u):
        sl = self.u_next
        self.u_next = (self.u_next + 1) % self.NU
        self.DMA("sp", self.uring[sl][:].rearrange("p a b -> p (a b)"), dram_units[u],
                 [self.cur_cast], [self.B_uring[sl]], "ld_u%d" % sl)
        return self.uring[sl], self.B_uring[sl]

    def load_panel(self, dram_panels, dq, kc0, nk, KCt):
        sl = self.p_next
        self.p_next = (self.p_next + 1) % 2
        src = dram_panels[dq].rearrange("p (k c) -> p k c", c=512)[:, kc0:kc0 + nk, :]
        self.DMA("sp", self.pring[sl][:, 0:nk, :], src, [self.cur_cast], [self.B_pring[sl]], "ld_p%d" % sl)
        return self.pring[sl], self.B_pring[sl]

    def stream_rows(self, s, t, blk):
        r0 = s * self.S + t * T + blk * 128
        return slice(r0, r0 + 128)

    def prologue_block(self, k, s, blk, slot, which):
        xb, Bx = self.xblk[slot], self.B_xblk[slot]
        ws = [self.B_work[4 * blk + i] for i in range(4)]
        junk = self.work[:, 4 * blk:4 * blk + 4, :].rearrange("p a b -> p (a b)")
        ssq = self.small[:, 16 + blk:17 + blk]
        Bs = self.B_small[1 + blk]
        self.ACT(junk, xb[:], AF.Square, [Bx], ws + [Bs], accum_out=ssq)
        self.ACT(ssq, ssq, AF.Sqrt, [], [Bs], scale=1.0 / D, bias=EPS)
        self.sc.op("dve", lambda e: e.reciprocal(out=ssq, in_=ssq), w=[Bs])
        self.TS("dve", junk, xb[:], ssq, None, ALU.mult, None, [Bx, Bs], ws)
        pb = 2 + (blk % 2)
        for g4 in range(4):
            for j in range(4):
                kc = g4 * 4 + j
                self.TR(self.ps[pb][:, j * 128:(j + 1) * 128], junk[:, kc * 128:(kc + 1) * 128], self.ident_f,
                        ws + [self.B_const], [self.B_ps[pb]])
            for j in range(4):
                kc = g4 * 4 + j
                o = self.hT[:, kc, blk * 128:(blk + 1) * 128]
                i = self.ps[pb][:, j * 128:(j + 1) * 128]
                A = self.AB[:, k, s, which, kc:kc + 1]
                Bv = self.AB[:, k, s, which + 1, kc:kc + 1]
                if (kc % 2) == 0:
                    self.ACT(o, i, AF.Identity, [self.B_AB], [self.B_ps[pb], self.B_hT], scale=A, bias=Bv)
                else:
                    self.TS("dve", o, i, A, Bv, ALU.mult, ALU.add, [self.B_AB], [self.B_ps[pb], self.B_hT])

    def prologue_tile(self, k, s, t, src):
        for blk in range(NBLK):
            slot = blk % 2
            self.DMA("sp", self.xblk[slot][:], src[self.stream_rows(s, t, blk), :],
                     [self.B_stream[s][t]], [self.B_xblk[slot]], "ld_x%d" % slot)
            self.prologue_block(k, s, blk, slot, 0)

    def epilogue_tile(self, k, s, t, sub, panels, KCt, src, next_prologue):
        grow = (k * 2 + sub) * self.NSEQ + s
        self.DMA("sp", self.gbc[:], self.gvec[grow:grow + 1, :].to_broadcast([128, D]),
                 [self.B_gvec], [self.B_gbc], "ld_g")
        ssqp = self.small[:, 32:48].rearrange("p (b q) -> p b q", q=4)
        Bq = self.B_small[6]
        step = 8
        for dq in range(4):
            for kc0 in range(0, KCt, step):
                nk = min(step, KCt - kc0)
                pt, Bp = self.load_panel(panels, dq, kc0, nk, KCt)
                for blk in range(NBLK):
                    for j in range(nk):
                        kc = kc0 + j
                        self.MM(self.ps[4 + blk][:], self.hid[:, kc, blk * 128:(blk + 1) * 128], pt[:, j, :],
                                kc == 0, kc == KCt - 1, [self.B_hid[kc], Bp], [self.B_ps[4 + blk]])
            for blk in range(NBLK):
                y = self.work[:, 4 * blk + dq, :]
                By = self.B_work[4 * blk + dq]
                self.ACT(y, self.ps[4 + blk][:], AF.Copy, [], [self.B_ps[4 + blk], By])
                self.STT(self.wslot(16, BF16)[:, 0:512], y, 1.0, y, ALU.mult, ALU.mult, [By], [self.B_work[16], Bq],
                         accum_out=ssqp[:, blk, dq:dq + 1])
        for blk in range(NBLK):
            slot = blk % 2
            xb, Bx = self.xblk[slot], self.B_xblk[slot]
            ws = [self.B_work[4 * blk + i] for i in range(4)]
            yb = self.work[:, 4 * blk:4 * blk + 4, :].rearrange("p a b -> p (a b)")
            self.DMA("sp", xb[:], src[self.stream_rows(s, t, blk), :], [self.B_stream[s][t]], [Bx], "ld_x%d" % slot)
            rs = self.small[:, 20 + blk:21 + blk]
            Br = self.B_small[7 + blk]
            self.sc.op("dve", lambda e, rs=rs, blk=blk: e.tensor_reduce(out=rs, in_=ssqp[:, blk, :], axis=mybir.AxisListType.X, op=ALU.add),
                       r=[Bq], w=[Br])
            self.ACT(rs, rs, AF.Sqrt, [], [Br], scale=1.0 / D, bias=EPS)
            self.sc.op("dve", lambda e, rs=rs: e.reciprocal(out=rs, in_=rs), w=[Br])
            self.STT(yb, yb, rs, self.gbc[:], ALU.mult, ALU.mult, [Br, self.B_gbc], ws)
            self.TT("pool", xb[:], xb[:], yb, ALU.add, ws, [Bx])
            self.DMA("pool", self.out[self.stream_rows(s, t, blk), :], xb[:], [Bx], [self.B_stream[s][t]], "st_x%d" % slot)
            if next_prologue:
                self.prologue_block(k, s, blk, slot, 2)

    def ffn_tile(self, k, s, t):
        li = self.layer_ids[k]
        w = self.W[li]
        for m in range(FC):
            gt, Bg = self.load_unit(w["ffn_gu_b"], 2 * m)
            ut, Bu = self.load_unit(w["ffn_gu_b"], 2 * m + 1)
            pg, pu = (m % 2), 2 + (m % 2)
            for kc in range(KC):
                self.MM(self.ps[pg][:], gt[:, kc, :], self.hT[:, kc, :], kc == 0, kc == KC - 1,
                        [Bg, self.B_hT], [self.B_ps[pg]])
            for kc in range(KC):
                self.MM(self.ps[pu][:], ut[:, kc, :], self.hT[:, kc, :], kc == 0, kc == KC - 1,
                        [Bu, self.B_hT], [self.B_ps[pu]])
            tmp = self.work[:, 16 + (m % 2), :]
            Bt = self.B_work[16 + (m % 2)]
            self.ACT(tmp, self.ps[pg][:], AF.Silu, [], [self.B_ps[pg], Bt])
            self.TT("dve", self.hid[:, m, :], self.ps[pu][:], tmp, ALU.mult, [Bt], [self.B_ps[pu], self.B_hid[m]])

    def hg_setup(self, k):
        li = self.layer_ids[k]
        w = self.W[li]
        Bp = self.B_par
        if not hasattr(self, "hg_par"):
            self.hg_par = self.sb("hg_par", [128, 4, 16], F32)
            self.hg_lg = self.sb("hg_lg", [128, 32], F32)
            self.hg_nw = self.sb("hg_nw", [128, 1], F32)
        self.DMA("sp", self.hg_lg[:], w["lbT"], [], [Bp], "ld_small")
        self.DMA("sp", self.hg_nw[:], w["hnw"], [], [Bp], "ld_small")
        lb, oml, noml = self.hg_par[:, 0], self.hg_par[:, 1], self.hg_par[:, 2]
        if li == 0:
            self.MEMSET("dve", lb, 0.0, [Bp])
        else:
            self.TT("dve", lb, self.hg_lg[:, 16:32], self.hg_lg[:, 0:16], ALU.subtract, [], [Bp])
            self.ACT(lb, lb, AF.Sigmoid, [], [Bp])
        self.TS("dve", oml, lb, -1.0, 1.0, ALU.mult, ALU.add, [], [Bp])
        self.TS("dve", noml, oml, -1.0, None, ALU.mult, None, [], [Bp])

    def hg_tile(self, k, s, t):
        li = self.layer_ids[k]
        w = self.W[li]
        units = w["in_b"]
        Bp, Bc = self.B_par, self.B_hgc
        W_ = self.work
        BW = self.B_work
        QS, SIG, G, KF, B_, E1, E2, DIF, GS, OSB, RS = 0, 1, 2, 3, 4, 5, 6, 7, 8, 9, 10
        QT, KT, KDT, KDV, VBM0, VBM1, MISC = 11, 12, 13, 14, 15, 16, 17
        qt = self.wslot(QT, BF16)[:, 0:T]
        kt = self.wslot(KT, BF16)[:, 0:T]
        kdT = self.wslot(KDT, BF16)[:, 0:T]
        kd_tok = self.wslot(KDV, BF16)[:, 0:512].rearrange("p (b c) -> p b c", c=128)
        v_tok = self.wslot(KDV, BF16)[:, 512:1024].rearrange("p (b c) -> p b c", c=128)
        vbm = self.work[:, VBM0:VBM1 + 1, :].rearrange("p a b -> p (a b)").bitcast(BF16).rearrange("p (b n c) -> p b n c", n=4, c=128)
        Bvbm = [BW[VBM0], BW[VBM1]]
        misc = self.wslot(MISC, BF16)
        scm = [misc[:, 0:128], misc[:, 128:256]]
        sbf = [misc[:, 256:384], misc[:, 384:512]]
        osq = misc[:, 512:1024]
        dn = self.small[:, 48:64]
        Bdn = self.B_small[12]
        for h in range(16):
            lbh, omlh, nomlh = self.hg_par[:, 0, h:h + 1], self.hg_par[:, 1, h:h + 1], self.hg_par[:, 2, h:h + 1]
            S_f = self.state[:, h, :]
            BS = self.B_state[h]
            if t == 0:
                self.MEMSET("pool", S_f, 0.0, [BS])
            ut, Bu = self.load_unit(units, 4 * h + 0)
            pq = 0
            for kc in range(KC):
                self.MM(self.ps[pq][:], ut[:, kc, :], self.hT[:, kc, :], kc == 0, kc == KC - 1, [Bu, self.B_hT], [self.B_ps[pq]])
            self.ACT(W_[:, QS, :], self.ps[pq][:], AF.Silu, [], [self.B_ps[pq], BW[QS]])
            ut, Bu = self.load_unit(units, 4 * h + 1)
            pf = 1
            for kc in range(KC):
                self.MM(self.ps[pf][:], ut[:, kc, :], self.hT[:, kc, :], kc == 0, kc == KC - 1, [Bu, self.B_hT], [self.B_ps[pf]])
            self.ACT(W_[:, SIG, :], self.ps[pf][:], AF.Sigmoid, [], [self.B_ps[pf], BW[SIG]])
            self.ACT(W_[:, G, :], W_[:, SIG, :], AF.Ln, [BW[SIG], Bp], [BW[G]], scale=omlh, bias=lbh)
            self.TS("dve", W_[:, KF, :], W_[:, SIG, :], nomlh, omlh, ALU.mult, ALU.add, [BW[SIG], Bp], [BW[KF]])
            self.sc.op("dve", lambda e: e.tensor_tensor_scan(out=W_[:, B_, :], data0=self.hg_reset, data1=W_[:, G, :],
                                                             initial=0.0, op0=ALU.mult, op1=ALU.add),
                       r=[Bc, BW[G]], w=[BW[B_]])
            self.ACT(W_[:, E1, :], W_[:, B_, :], AF.Exp, [BW[B_]], [BW[E1]])
            self.TT("dve", qt, W_[:, QS, :], W_[:, E1, :], ALU.mult, [BW[QS], BW[E1]], [BW[QT]])
            self.ACT(W_[:, E2, :], W_[:, B_, :], AF.Exp, [BW[B_]], [BW[E2]], scale=-1.0)
            self.TT("dve", kt, W_[:, KF, :], W_[:, E2, :], ALU.mult, [BW[KF], BW[E2]], [BW[KT]])
            b3 = W_[:, B_, :].rearrange("p (c k) -> p c k", k=32)
            self.TT("dve", W_[:, DIF, :].rearrange("p (c k) -> p c k", k=32), b3[:, :, 31:32].to_broadcast([128, 16, 32]), b3,
                    ALU.subtract, [BW[B_]], [BW[DIF]])
            self.ACT(W_[:, DIF, :], W_[:, DIF, :], AF.Exp, [], [BW[DIF]])
            self.TT("dve", kdT, W_[:, KF, :], W_[:, DIF, :], ALU.mult, [BW[KF], BW[DIF]], [BW[KDT]])
            self.ACT(dn, b3[:, :, 31], AF.Exp, [BW[B_]], [Bdn])
            pm = 2
            pmb = self.ps[pm][:].bitcast(BF16)
            for blk in range(NBLK):
                self.TR(pmb[:, blk * 128:(blk + 1) * 128], kdT[:, blk * 128:(blk + 1) * 128], self.ident_b[:],
                        [BW[KDT], self.B_const], [self.B_ps[pm]])
            self.CP("act", kd_tok.rearrange("p b c -> p (b c)"), pmb[:, 0:512], [], [self.B_ps[pm], BW[KDV]])
            ut, Bu = self.load_unit(units, 4 * h + 2)
            pv = 0
            for blk in range(NBLK):
                for kc in range(KC):
                    self.MM(self.ps[pv][:, blk * 128:(blk + 1) * 128], self.hT[:, kc, blk * 128:(blk + 1) * 128], ut[:, kc, :],
                            kc == 0, kc == KC - 1, [Bu, self.B_hT], [self.B_ps[pv]])
            self.CP("act", v_tok.rearrange("p b c -> p (b c)"), self.ps[pv][:], [], [self.B_ps[pv], BW[KDV]])
            for blk in range(NBLK):
                self.TT("dve", vbm[:, blk], self.ps[pv][:, blk * 128:(blk + 1) * 128].unsqueeze(1).to_broadcast([128, 4, 128]),
                        self.hg_vmask, ALU.mult, [Bc], [self.B_ps[pv], Bvbm[blk // 2]])
            ut, Bu = self.load_unit(units, 4 * h + 3)
            pg = 1
            for kc in range(KC):
                self.MM(self.ps[pg][:], ut[:, kc, :], self.hT[:, kc, :], kc == 0, kc == KC - 1, [Bu, self.B_hT], [self.B_ps[pg]])
            self.ACT(W_[:, GS, :], self.ps[pg][:], AF.Sigmoid, [], [self.B_ps[pg], BW[GS]])
            po, pu_ = 4, 3
            self.CP("act", sbf[0], S_f, [BS], [BW[MISC]])
            cur = 0
            for blk in range(NBLK):
                c0 = blk * 128
                self.MM(self.ps[pm][:, 0:128], kt[:, c0:c0 + 128], qt[:, c0:c0 + 128], True, True, [BW[KT], BW[QT]], [self.B_ps[pm]])
                self.TT("dve", scm[blk % 2], self.ps[pm][:, 0:128], self.hg_mask, ALU.mult, [Bc], [self.B_ps[pm], BW[MISC]])
                self.MM(self.ps[pu_][:], kd_tok[:, blk, :], vbm[:, blk].rearrange("p n c -> p (n c)"), True, True,
                        [BW[KDV], Bvbm[blk // 2]], [self.B_ps[pu_]])
                self.MM(self.ps[po][:, c0:c0 + 128], v_tok[:, blk, :], scm[blk % 2], True, False, [BW[KDV], BW[MISC]], [self.B_ps[po]])
                for n in range(4):
                    c = blk * 4 + n
                    self.MM(self.ps[po][:, c0 + 32 * n:c0 + 32 * n + 32], sbf[cur], qt[:, c0 + 32 * n:c0 + 32 * n + 32],
                            False, n == 3, [BW[MISC], BW[QT]], [self.B_ps[po]])
                    self.STT(S_f, S_f, dn[:, c:c + 1], self.ps[pu_][:, n * 128:(n + 1) * 128], ALU.mult, ALU.add,
                             [Bdn], [self.B_ps[pu_], BS])
                    cur ^= 1
                    self.CP("act", sbf[cur], S_f, [BS], [BW[MISC]])
            self.ACT(osq, self.ps[po][:], AF.Square, [], [self.B_ps[po], BW[MISC]])
            self.CP("dve", W_[:, OSB, :], self.ps[po][:], [], [self.B_ps[po], BW[OSB]])
            self.MM(self.ps[pm][:], self.ones_b[:], osq, True, True, [self.B_const, BW[MISC]], [self.B_ps[pm]])
            self.ACT(W_[:, RS, :], self.ps[pm][:], AF.Sqrt, [], [self.B_ps[pm], BW[RS]], bias=EPS)
            self.sc.op("dve", lambda e: e.reciprocal(out=W_[:, RS, :], in_=W_[:, RS, :]), w=[BW[RS]])
            self.STT(W_[:, OSB, :], W_[:, OSB, :], self.hg_nw[:, 0:1], W_[:, RS, :], ALU.mult, ALU.mult, [Bp, BW[RS]], [BW[OSB]])
            self.TT("dve", self.hid[:, h, :], W_[:, OSB, :], W_[:, GS, :], ALU.mult, [BW[OSB], BW[GS]], [self.B_hid[h]])

    def body(self):
        NSEQ = self.NSEQ
        nl = len(self.layer_ids)
        for li in self.layer_ids[:1]:
            self.cast_layer(li)
        self.adaln()
        for k, li in enumerate(self.layer_ids):
            w = self.W[li]
            if k + 1 < nl:
                self.cast_layer(self.layer_ids[k + 1])
            self.cur_cast = w["B_cast"]
            if li % 2 == 0:
                self.hg_setup(k)
            else:
                self.gdn_setup(k)
            for s in range(NSEQ):
                for t in range(self.NT):
                    src = self.x_in if k == 0 else self.out
                    self.prologue_tile(k, s, t, src)
                    if li % 2 == 0:
                        self.hg_tile(k, s, t)
                        KCt = 16
                    else:
                        self.gdn_tile(k, s, t)
                        KCt = 32
                    last = (k == nl - 1) and self.stop_after == "mixer"
                    self.epilogue_tile(k, s, t, 0, w["out_b"], KCt, src, not last)
                    if last:
                        continue
                    self.ffn_tile(k, s, t)
                    self.epilogue_tile(k, s, t, 1, w["ffn_dn_b"], FC, self.out, False)
        allb = [b for row in self.B_stream for b in row]
        self.sc.wait_all("pool", allb)
        self.sc.wait_all("sp", allb)

    def gdn_setup(self, k):
        li = self.layer_ids[k]
        w = self.W[li]
        Bp = self.B_par
        if not hasattr(self, "g_conv"):
            self.g_conv = self.sb("g_conv", [128, 64, 4], F32)
            self.g_aldt = self.sb("g_aldt", [128, 64], F32)
            self.g_negA = self.sb("g_negA", [128, 32], F32)
            self.g_nw = self.sb("g_nw", [128, 1], F32)
            self.g_ba = self.sb("g_ba", [128, KC, 64], BF16)
            self.g_carry = self.sb("g_carry", [128, 64, 3], F32)
            self.B_carry = [Buf("carry%d" % i) for i in range(64)]
            self.gsm = self.sb("gsm", [128, 6, 4, 32], F32)
            self.B_gsm = Buf("gsm")
            self.gw_f2 = [self.sb("gw_f%d" % h, [128, 12, 128], F32) for h in range(2)]
            self.B_gwf2 = [[Buf("gwf%d_%d" % (h, i)) for i in range(14)] for h in range(2)]
            self.gw_b2 = [self.sb("gw_b%d" % h, [128, 10, 128], BF16) for h in range(2)]
            self.B_gwb2 = [[Buf("gwb%d_%d" % (h, i)) for i in range(16)] for h in range(2)]
            self.B_gba = Buf("gba")
            self.kkqk_sb = self.sb("kkqk_sb", [128, 4, 256], F32)
            self.B_kkqk = [Buf("kkqk%d" % i) for i in range(4)]
            self.g_tri = self.cst[:, 1280:1408]
            self.g_blk1 = self.cst[:, 1408:1536]
            self.g_neg = self.cst[:, 1536:1664]
            self.g_pos = self.cst[:, 1664:1792]
            self.ones_f = self.cst[:, 1792:1920]
            self.pm_rot = 0
        self.DMA("sp", self.g_conv[:].rearrange("p a b -> p (a b)"), w["convT"], [], [Bp], "ld_small")
        self.DMA("sp", self.g_aldt[:], w["aldt"], [], [Bp], "ld_small")
        self.DMA("sp", self.g_nw[:], w["gnw"], [], [Bp], "ld_small")
        self.DMA("sp", self.g_ba[:].rearrange("p a b -> p (a b)"), w["ba_b"], [w["B_cast"]], [self.B_gba], "ld_small")
        self.ACT(self.g_negA[:], self.g_aldt[:, 0:32], AF.Exp, [], [Bp])
        self.TS("dve", self.g_negA[:], self.g_negA[:], -1.0, None, ALU.mult, None, [], [Bp])

    def pm(self):
        banks = (2, 3, 6, 7)
        self.pm_rot = (self.pm_rot + 1) % len(banks)
        b = banks[self.pm_rot]
        return self.ps[b], self.B_ps[b]

    def gdn_tile(self, k, s, t):
        li = self.layer_ids[k]
        w = self.W[li]
        units = w["in_b"]
        Bp, Bc = self.B_par, self.B_const
        W_, BW = self.work, self.B_work
        NLEV = GDN_NLEV
        XP0, XP1, ACC, RINV, SQ, QK, KV, ZS, VT, OSB, RS, OSQ = 0, 1, 2, 3, 4, 5, 6, 7, 8, 9, 10, 11
        xpad = self.work[:, XP0:XP1 + 1, :].rearrange("p a b -> p (a b)")[:, 0:T + 3]
        Bxp = [BW[XP0], BW[XP1]]
        acc = W_[:, ACC, :]
        sq = self.wslot(SQ, BF16)[:, 0:T]
        qT = self.wslot(QK, BF16)[:, 0:T]
        kT = self.wslot(QK, BF16)[:, T:2 * T]
        k_tok = self.wslot(KV, BF16)[:, 0:512].rearrange("p (b c) -> p b c", c=128)
        v_tok = self.wslot(KV, BF16)[:, 512:1024].rearrange("p (b c) -> p b c", c=128)
        zs = self.wslot(ZS, BF16)[:, 0:T]
        vT = self.wslot(VT, BF16)[:, 0:T]
        osq = self.wslot(OSQ, BF16)[:, 0:T]
        gsm = self.gsm
        BET, GTK, GCS, TOT, BK, EKD = 0, 1, 2, 3, 4, 5
        Bg = self.B_gsm
        DIAG, E1, E2, EGC, USB = range(5)
        ATT, TT_A, VB, KBG, KD, WT, QG, VN, SB0, SB1 = range(10)

        if t == 0:
            self.MEMSET("pool", self.g_carry[:], 0.0, self.B_carry)
            for vh in range(32):
                self.MEMSET("pool", self.state[:, vh, :], 0.0, [self.B_state[vh]])

        pb, Bpb = self.ps[0], self.B_ps[0]
        for blk in range(NBLK):
            for kc in range(KC):
                self.MM(pb[:, blk * 64:(blk + 1) * 64], self.hT[:, kc, blk * 128:(blk + 1) * 128], self.g_ba[:, kc, :],
                        kc == 0, kc == KC - 1, [self.B_hT, self.B_gba], [Bpb])
        pb3 = pb[:, 0:256].rearrange("p (b c) -> p b c", c=64)
        self.ACT(gsm[:, BET], pb3[:, :, 0:32], AF.Sigmoid, [], [Bpb, Bg])
        self.TT("dve", gsm[:, GTK], pb3[:, :, 32:64], self.g_aldt[:, 32:64].unsqueeze(1).to_broadcast([128, 4, 32]), ALU.add,
                [Bp], [Bpb, Bg])
        self.ACT(gsm[:, GTK], gsm[:, GTK], AF.Exp, [], [Bg])
        self.ACT(gsm[:, GTK], gsm[:, GTK], AF.Ln, [], [Bg], bias=1.0)
        self.TT("dve", gsm[:, GTK], gsm[:, GTK], self.g_negA[:].unsqueeze(1).to_broadcast([128, 4, 32]), ALU.mult, [Bp], [Bg])
        pc, Bpc = self.ps[1], self.B_ps[1]
        for blk in range(NBLK):
            self.MM(pc[:, blk * 32:(blk + 1) * 32], self.g_tri, gsm[:, GTK, blk, :], True, True, [Bc, Bg], [Bpc])
            self.MM(pc[:, 128 + blk * 32:128 + (blk + 1) * 32], self.g_blk1, gsm[:, GTK, blk, :], True, True, [Bc, Bg], [Bpc])
        self.CP("dve", gsm[:, GCS].rearrange("p b c -> p (b c)"), pc[:, 0:128], [], [Bpc, Bg])
        self.CP("dve", gsm[:, TOT].rearrange("p b c -> p (b c)"), pc[:, 128:256], [], [Bpc, Bg])
        self.ACT(gsm[:, BK], gsm[:, GCS], AF.Exp, [], [Bg])
        self.TT("dve", gsm[:, BK], gsm[:, BK], gsm[:, BET], ALU.mult, [], [Bg])
        self.TT("dve", gsm[:, EKD], gsm[:, TOT], gsm[:, GCS], ALU.subtract, [], [Bg])
        self.ACT(gsm[:, EKD], gsm[:, EKD], AF.Exp, [], [Bg])

        def proj_fm(u, pbank):
            ut, Bu = self.load_unit(units, u)
            for kc in range(KC):
                self.MM(self.ps[pbank][:], ut[:, kc, :], self.hT[:, kc, :], kc == 0, kc == KC - 1, [Bu, self.B_hT], [self.B_ps[pbank]])

        def conv_silu(pbank, cc, out_ap, out_bufs):
            Bcar = self.B_carry[cc]
            self.CP("dve", xpad[:, 0:3], self.g_carry[:, cc, :], [Bcar], Bxp)
            self.ACT(xpad[:, 3:T + 3], self.ps[pbank][:], AF.Copy, [], [self.B_ps[pbank]] + Bxp)
            self.CP("dve", self.g_carry[:, cc, :], xpad[:, T:T + 3], Bxp, [Bcar])
            cw = self.g_conv[:, cc, :]
            self.TS("dve", acc, xpad[:, 3:T + 3], cw[:, 3:4], None, ALU.mult, None, Bxp + [Bp], [BW[ACC]])
            for j in (2, 1, 0):
                self.STT(acc, xpad[:, j:j + T], cw[:, j:j + 1], acc, ALU.mult, ALU.add, Bxp + [Bp], [BW[ACC]])
            self.ACT(out_ap, acc, AF.Silu, [BW[ACC]], out_bufs)

        def l2norm(dst, scale):
            self.ACT(sq, acc, AF.Square, [BW[ACC]], [BW[SQ]])
            pt, Bpt = self.pm()
            self.MM(pt[:], self.ones1_b[:], sq, True, True, [Bc, BW[SQ]], [Bpt])
            self.ACT(W_[:, RINV, :], pt[:], AF.Sqrt, [], [Bpt, BW[RINV]], bias=EPS)
            self.sc.op("dve", lambda e: e.reciprocal(out=W_[:, RINV, :], in_=W_[:, RINV, :]), w=[BW[RINV]])
            self.STT(dst, acc, scale, W_[:, RINV, :], ALU.mult, ALU.mult, [BW[ACC], BW[RINV]], [BW[QK]])

        def to_tok(srcT, dst, dst_bufs, src_bufs):
            pt, Bpt = self.pm()
            ptb = pt[:].bitcast(BF16)
            for blk in range(NBLK):
                self.TR(ptb[:, blk * 128:(blk + 1) * 128], srcT[:, blk * 128:(blk + 1) * 128], self.ident_b[:], src_bufs + [Bc], [Bpt])
            self.CP("act", dst.rearrange("p b c -> p (b c)"), ptb[:, 0:512], [], [Bpt] + dst_bufs)

        for g in range(16):
            proj_fm(6 * g + 0, 0)
            conv_silu(0, g, acc, [BW[ACC]])
            l2norm(qT, 128.0 ** -0.5)
            proj_fm(6 * g + 1, 1)
            conv_silu(1, 16 + g, acc, [BW[ACC]])
            l2norm(kT, 1.0)
            to_tok(kT, k_tok, [BW[KV]], [BW[QK]])
            for blk in range(NBLK):
                c0 = blk * 128
                pt, Bpt = self.pm()
                self.MM(pt[:, 0:128], kT[:, c0:c0 + 128], kT[:, c0:c0 + 128], True, True, [BW[QK]], [Bpt])
                self.MM(pt[:, 128:256], kT[:, c0:c0 + 128], qT[:, c0:c0 + 128], True, True, [BW[QK]], [Bpt])
                self.CP("act", self.kkqk_sb[:, blk, :], pt[:, 0:256], [], [Bpt, self.B_kkqk[blk]])
            VTOK = [(KV, 512), (12, 0)]
            ZSL = [ZS, 13]
            for hh in range(2):
                vh = 2 * g + hh
                vtk = self.wslot(VTOK[hh][0], BF16)[:, VTOK[hh][1]:VTOK[hh][1] + 512].rearrange("p (b c) -> p b c", c=128)
                proj_fm(6 * g + 2 + 2 * hh, 0)
                conv_silu(0, 32 + vh, vT, [BW[VT]])
                to_tok(vT, vtk, [BW[VTOK[hh][0]]], [BW[VT]])
                proj_fm(6 * g + 3 + 2 * hh, 1)
                self.ACT(self.wslot(ZSL[hh], BF16)[:, 0:T], self.ps[1][:], AF.Silu, [], [self.B_ps[1], BW[ZSL[hh]]])

            def head_chain(hh):
                vh = 2 * g + hh
                S_f = self.state[:, vh, :]
                BS = self.B_state[vh]
                gw_f, gw_b = self.gw_f2[hh], self.gw_b2[hh]
                Bf, Bb = self.B_gwf2[hh], self.B_gwb2[hh]
                gwf = lambda i: gw_f[:, i, :]
                gwb = lambda i: gw_b[:, i, :]
                v_tok = self.wslot(VTOK[hh][0], BF16)[:, VTOK[hh][1]:VTOK[hh][1] + 512].rearrange("p (b c) -> p b c", c=128)
                BVT = BW[VTOK[hh][0]]
                zs_h = self.wslot(ZSL[hh], BF16)[:, 0:T]
                BZ = BW[ZSL[hh]]
                OSB_h, RS_h, OSQ_h = (OSB, RS, OSQ) if hh == 0 else (15, 16, 17)
                osq_h = self.wslot(OSQ_h, BF16)[:, 0:T]
                po, Bpo = self.ps[4 + hh], self.B_ps[4 + hh]
                self.CP("act", gwb(SB0), S_f, [BS], [Bb[SB0]])
                cur = SB0
                yield
                for blk in range(NBLK):
                    c0 = blk * 128
                    kk_sb = self.kkqk_sb[:, blk, 0:128]
                    qk_sb = self.kkqk_sb[:, blk, 128:256]
                    Bkq = self.B_kkqk[blk]
                    gcs = gsm[:, GCS, blk, vh:vh + 1]
                    bet = gsm[:, BET, blk, vh:vh + 1]
                    bk = gsm[:, BK, blk, vh:vh + 1]
                    ekd = gsm[:, EKD, blk, vh:vh + 1]
                    self.TS("pool", gwf(DIAG), self.ident_f, gcs, None, ALU.mult, None, [Bc, Bg], [Bf[DIAG]])
                    self.TS("pool", gwb(VB), v_tok[:, blk, :], bet, None, ALU.mult, None, [BVT, Bg], [Bb[VB]])
                    self.TS("pool", gwb(KBG), k_tok[:, blk, :], bk, None, ALU.mult, None, [BW[KV], Bg], [Bb[KBG]])
                    self.TS("pool", gwb(KD), k_tok[:, blk, :], ekd, None, ALU.mult, None, [BW[KV], Bg], [Bb[KD]])
                    yield
                    pg, Bpg = self.pm()
                    self.MM(pg[:, 0:128], self.ones_f, gwf(DIAG), True, True, [Bc, Bf[DIAG]], [Bpg])
                    yield
                    self.STT(gwf(E2), pg[:, 0:128], gcs, self.g_pos, ALU.subtract, ALU.add, [Bg, Bc], [Bpg, Bf[E2]])
                    self.STT(gwf(E1), pg[:, 0:128], gcs, self.g_neg, ALU.subtract, ALU.add, [Bg, Bc], [Bpg, Bf[E1]])
                    self.ACT(gwf(EGC), pg[:, 0:128], AF.Exp, [], [Bpg, Bf[EGC]])
                    yield
                    self.ACT(gwf(E2), gwf(E2), AF.Exp, [], [Bf[E2]], scale=-1.0)
                    self.ACT(gwf(E1), gwf(E1), AF.Exp, [], [Bf[E1]])
                    yield
                    A0F, PF_A, PF_B, PTF_A, PTF_B, TTF_A, TTF_B = range(5, 12)
                    self.STT(gwf(A0F), kk_sb, bet, gwf(E2), ALU.mult, ALU.mult, [Bkq, Bg, Bf[E2]], [Bf[A0F]])
                    self.TT("pool", gwb(ATT), qk_sb, gwf(E1), ALU.mult, [Bkq, Bf[E1]], [Bb[ATT]])
                    self.TT("dve", gwb(QG), qT[:, c0:c0 + 128], gwf(EGC), ALU.mult, [BW[QK], Bf[EGC]], [Bb[QG]])
                    yield
                    pa, Bpa = self.pm()
                    self.MM(pa[:, 0:128], gwf(A0F), self.ident_f, True, True, [Bf[A0F], Bc], [Bpa])
                    yield
                    self.STT(gwf(TTF_A), pa[:, 0:128], -1.0, self.ident_f, ALU.mult, ALU.add, [Bc], [Bpa, Bf[TTF_A]])
                    self.CP("act", gwf(PTF_A), pa[:, 0:128], [], [Bpa, Bf[PTF_A]])
                    yield
                    P, Pt, Tt = A0F, PTF_A, TTF_A
                    for lev in range(1, NLEV + 1):
                        Pn = PF_A if P != PF_A else PF_B
                        Ptn = PTF_A if Pt != PTF_A else PTF_B
                        Ttn = TTF_A if Tt != TTF_A else TTF_B
                        p1, Bp1 = self.pm()
                        self.MM(p1[:, 0:128], gwf(Pt), gwf(P), True, True, [Bf[Pt], Bf[P]], [Bp1])
                        if lev < NLEV:
                            self.MM(p1[:, 128:256], gwf(P), gwf(Pt), True, True, [Bf[Pt], Bf[P]], [Bp1])
                        yield
                        self.CP("act", gwf(Pn), p1[:, 0:128], [], [Bp1, Bf[Pn]])
                        if lev < NLEV:
                            self.CP("dve", gwf(Ptn), p1[:, 128:256], [], [Bp1, Bf[Ptn]])
                        yield
                        p2, Bp2 = self.pm()
                        self.MM(p2[:, 0:128], gwf(Pn), gwf(Tt), True, True, [Bf[Pn], Bf[Tt]], [Bp2])
                        yield
                        if lev < NLEV:
                            self.TT("dve", gwf(Ttn), p2[:, 0:128], gwf(Tt), ALU.add, [Bf[Tt]], [Bp2, Bf[Ttn]])
                        else:
                            self.TT("dve", gwb(TT_A), p2[:, 0:128], gwf(Tt), ALU.add, [Bf[Tt]], [Bp2, Bb[TT_A]])
                        yield
                        P, Pt, Tt = Pn, Ptn, Ttn
                    Tt = TT_A
                    pu, Bpu = self.pm()
                    self.MM(pu[:, 0:128], gwb(Tt), gwb(VB), True, True, [Bb[Tt], Bb[VB]], [Bpu])
                    self.MM(pu[:, 128:256], gwb(KBG), gwb(Tt), True, True, [Bb[Tt], Bb[KBG]], [Bpu])
                    yield
                    self.CP("act", gwf(USB), pu[:, 0:128], [], [Bpu, Bf[USB]])
                    self.CP("dve", gwb(WT), pu[:, 128:256], [], [Bpu, Bb[WT]])
                    yield
                    for c in range(2):
                        R = slice(64 * c, 64 * c + 64)
                        pw, Bpw = self.pm()
                        self.MM(pw[:, 0:128], gwb(WT), gwb(cur), True, True, [Bb[WT], Bb[cur]], [Bpw])
                        yield
                        self.TT("dve", gw_b[R, VN, :], gw_f[R, USB, :], pw[R, 0:128], ALU.subtract, [Bf[USB]], [Bpw, Bb[VN]])
                        yield
                        oc = po[:, c0 + 64 * c:c0 + 64 * c + 64]
                        self.MM(pw[:, 128:256], gw_b[R, KD, :], gw_b[R, VN, :], True, True, [Bb[KD], Bb[VN]], [Bpw])
                        self.MM(oc, gwb(cur), gw_b[:, QG, 64 * c:64 * c + 64], True, False, [Bb[cur], Bb[QG]], [Bpo])
                        self.MM(oc, gw_b[R, VN, :], gw_b[R, ATT, 64 * c:64 * c + 64], False, True, [Bb[VN], Bb[ATT]], [Bpo])
                        yield
                        self.STT(S_f, S_f, gw_f[:, EGC, 64 * c + 63:64 * c + 64], pw[:, 128:256], ALU.mult, ALU.add,
                                 [Bf[EGC]], [Bpw, BS])
                        yield
                        cur = SB1 if cur == SB0 else SB0
                        self.CP("act", gwb(cur), S_f, [BS], [Bb[cur]])
                        yield
                self.ACT(osq_h, po[:], AF.Square, [], [Bpo, BW[OSQ_h]])
                self.CP("dve", W_[:, OSB_h, :], po[:], [], [Bpo, BW[OSB_h]])
                yield
                pt, Bpt = self.pm()
                self.MM(pt[:], self.ones_b[:], osq_h, True, True, [Bc, BW[OSQ_h]], [Bpt])
                yield
                self.ACT(W_[:, RS_h, :], pt[:], AF.Sqrt, [], [Bpt, BW[RS_h]], bias=EPS)
                yield
                rs_ap = W_[:, RS_h, :]
                self.sc.op("dve", lambda e, rs_ap=rs_ap: e.reciprocal(out=rs_ap, in_=rs_ap), w=[BW[RS_h]])
                yield
                self.STT(W_[:, OSB_h, :], W_[:, OSB_h, :], self.g_nw[:, 0:1], W_[:, RS_h, :], ALU.mult, ALU.mult, [Bp, BW[RS_h]], [BW[OSB_h]])
                yield
                self.TT("dve", self.hid[:, vh, :], W_[:, OSB_h, :], zs_h, ALU.mult, [BW[OSB_h], BZ], [self.B_hid[vh]])

            gens = [head_chain(0), head_chain(1)]
            while gens:
                for gen in list(gens):
                    try:
                        next(gen)
                    except StopIteration:
                        gens.remove(gen)


def _units(W, cols):
    Wr = W.reshape(KC, 128, W.shape[1])
    out = np.empty((len(cols), 128, KC, 128), np.float32)
    for u, c0 in enumerate(cols):
        out[u] = Wr[:, :, c0:c0 + 128].transpose(1, 0, 2)
    return out.reshape(len(cols), 128, KC * 128)


def _panels(W):
    K = W.shape[0]
    return np.ascontiguousarray(W.reshape(K // 128, 128, 4, 512).transpose(2, 1, 0, 3)).reshape(4, 128, (K // 128) * 512)


def _fm(v):
    return np.ascontiguousarray(v.reshape(-1, 128).T)


def layout_weights(layer_ids, ada_w, ada_b, norm_w, hg_w_in, hg_lb_logits, hg_norm_w, hg_w_out,
                   gdn_w_in, gdn_conv_w, gdn_A_log, gdn_dt_bias, gdn_norm_w, gdn_w_out,
                   ffn_w_gate_up, ffn_w_down):
    m = {}
    for li in layer_ids:
        j = li // 2
        m["ada_w_%d" % li] = np.ascontiguousarray(ada_w[li])
        m["ada_bT_%d" % li] = _fm(ada_b[li])
        m["normT_%d" % li] = np.concatenate([_fm(norm_w[li, i]) for i in range(4)], axis=1)
        cols = []
        for mm in range(FC):
            cols += [mm * 128, FFN_H + mm * 128]
        m["ffn_gu_%d" % li] = _units(ffn_w_gate_up[li], cols)
        m["ffn_dn_%d" % li] = _panels(ffn_w_down[li])
        if li % 2 == 0:
            cols = []
            for h in range(16):
                cols += [h * 128, 2048 + h * 128, 4096 + h * 128, 6144 + h * 128]
            m["hg_in_%d" % li] = _units(hg_w_in[j], cols)
            m["hg_out_%d" % li] = _panels(hg_w_out[j])
            m["hg_lbT_%d" % li] = np.concatenate([_fm(hg_lb_logits[0]), _fm(hg_lb_logits[j])], axis=1)
            m["hg_nw_%d" % li] = np.ascontiguousarray(hg_norm_w[j].reshape(128, 1))
        else:
            cols = []
            for g in range(16):
                cols += [g * 128, 2048 + g * 128, 4096 + (2 * g) * 128, 8192 + (2 * g) * 128,
                         4096 + (2 * g + 1) * 128, 8192 + (2 * g + 1) * 128]
            m["gdn_in_%d" % li] = _units(gdn_w_in[j], cols)
            ba = gdn_w_in[j][:, 12288:12352].reshape(KC, 128, 64).transpose(1, 0, 2)
            m["gdn_ba_%d" % li] = np.ascontiguousarray(ba).reshape(128, KC * 64)
            m["gdn_out_%d" % li] = _panels(gdn_w_out[j])
            cw = gdn_conv_w[j]
            m["gdn_convT_%d" % li] = np.ascontiguousarray(cw.reshape(4, 64, 128).transpose(2, 1, 0)).reshape(128, 256)
            al = np.concatenate([gdn_A_log[j], gdn_dt_bias[j]])[None, :]
            m["gdn_aldt_%d" % li] = np.ascontiguousarray(np.broadcast_to(al, (128, 64)))
            m["gdn_nw_%d" % li] = np.ascontiguousarray(gdn_norm_w[j].reshape(128, 1))
    return {k: np.ascontiguousarray(v, dtype=np.float32) for k, v in m.items()}


def host_consts():
    c = np.zeros((128, CST_COLS), np.float32)
    c[:, 0:128] = np.eye(128, dtype=np.float32)
    p = np.arange(128)
    c[:, 128:256] = ((p[:, None] // 32 == p[None, :] // 32) & (p[:, None] <= p[None, :])).astype(np.float32)
    r = np.ones(512, np.float32); r[0::32] = 0.0
    c[:, 256:768] = r[None, :]
    vm = np.zeros((128, 4, 128), np.float32)
    for n in range(4):
        vm[32 * n:32 * n + 32, n, :] = 1.0
    c[:, 768:1280] = vm.reshape(128, 512)
    same = (p[:, None] // 64 == p[None, :] // 64)
    c[:, 1280:1408] = (same & (p[:, None] <= p[None, :])).astype(np.float32)
    c[:, 1408:1536] = same.astype(np.float32)
    c[:, 1536:1664] = np.where(same & (p[:, None] <= p[None, :]), 0.0, -1.0e5)
    c[:, 1664:1792] = np.where(same & (p[None, :] < p[:, None]), 0.0, 1.0e5)
    c[:, 1792:1920] = 1.0
    return c


_PROG_CACHE = {}


def kernel(x, c, ada_w, ada_b, norm_w, hg_w_in, hg_lb_logits, hg_norm_w, hg_w_out,
           gdn_w_in, gdn_conv_w, gdn_A_log, gdn_dt_bias, gdn_norm_w, gdn_w_out,
           ffn_w_gate_up, ffn_w_down):
    x = np.asarray(x, np.float32)
    c = np.asarray(c, np.float32)
    Bt, S, _ = x.shape
    nseq = Bt // N_CORES
    layer_ids = list(range(DEPTH))
    key = (S, nseq, tuple(layer_ids))
    if key not in _PROG_CACHE:
        _PROG_CACHE[key] = Prog(S, nseq, layer_ids)
    prog = _PROG_CACHE[key]
    wm = layout_weights(layer_ids, *[np.asarray(a, np.float32) for a in (
        ada_w, ada_b, norm_w, hg_w_in, hg_lb_logits, hg_norm_w, hg_w_out, gdn_w_in, gdn_conv_w,
        gdn_A_log, gdn_dt_bias, gdn_norm_w, gdn_w_out, ffn_w_gate_up, ffn_w_down)])
    in_maps = []
    for core in range(N_CORES):
        xs = x[core * nseq:(core + 1) * nseq].reshape(nseq * S, D)
        cs = c[core * nseq:(core + 1) * nseq]
        cT = np.ascontiguousarray(cs.reshape(nseq, KC, 128).transpose(2, 1, 0))
        mp = {"x": np.ascontiguousarray(xs), "cT": cT, "cst": host_consts()}
        mp.update(wm)
        in_maps.append(mp)
    res = run_bass_kernel_spmd(prog.nc, in_maps, core_ids=list(range(N_CORES)))
    outs = [np.asarray(r["out"]).reshape(nseq, S, D) for r in res.results]
    return np.concatenate(outs, axis=0).astype(np.float32)
```

```python
import contextlib
import numpy as np
import concourse.bass as bass
import concourse.mybir as mybir
from concourse.bass_utils import run_bass_kernel_spmd

F32 = mybir.dt.float32
BF16 = mybir.dt.bfloat16
AF = mybir.ActivationFunctionType
ALU = mybir.AluOpType

SAME_ENG_SYNC = True
SEM_BLK = 30000
SEM_BLK_DMA = 30000

D = 2048
KC = 16
T = 512
NBLK = 4
FC = 44
FFN_H = 5632
EPS = 1e-6
N_CORES = 8
SEQ = 2048
DEPTH = 4
CST_COLS = 1920
GDN_NLEV = 5
GDN_INV_F32 = True
GDN_F32R = False
F32R = mybir.dt.float32r


class Buf:
    __slots__ = ("name", "lw", "readers")

    def __init__(self, name):
        self.name = name
        self.lw = None
        self.readers = {}


class _Op:
    __slots__ = ("waits", "fn", "needs_inc", "dma")

    def __init__(self, fn):
        self.waits = []
        self.fn = fn
        self.needs_inc = False
        self.dma = None


class Sched:
    def __init__(self, nc, stack):
        self.nc = nc
        self.stack = stack
        self.eng = {"pe": nc.tensor, "act": nc.scalar, "dve": nc.vector, "pool": nc.gpsimd, "sp": nc.sync}
        self.ops = {k: [] for k in self.eng}
        self.seen = {k: {} for k in self.eng}
        self.sem = {}
        self.semtot = {}
        self.dgen = {}

    def dsem(self, base):
        g = self.dgen.get(base, 0)
        name = "%s_g%d" % (base, g)
        if name in self.semtot and self.semtot[name] + 16 > SEM_BLK_DMA:
            g += 1
            name = "%s_g%d" % (base, g)
        self.dgen[base] = g
        if name not in self.sem:
            self.sem[name] = self.stack.enter_context(self.nc.semaphore(name))
            self.semtot[name] = 0
        return name

    def _need(self, e, op, ev):
        if ev is None:
            return
        if ev[0] == "e":
            f, idx = ev[1], ev[2]
            if f == e:
                if e in ("pe", "sp") or not SAME_ENG_SYNC:
                    return
            key = "e_" + f
            if self.seen[e].get(key, -1) >= idx:
                return
            self.seen[e][key] = idx
            self.ops[f][idx].needs_inc = True
            op.waits.append(ev)
        else:
            name = ev[1]
            tot = self.semtot[name]
            if self.seen[e].get(name, -1) >= tot:
                return
            self.seen[e][name] = tot
            op.waits.append(("d", name, tot))

    def op(self, e, fn, r=(), w=()):
        op = _Op(fn)
        idx = len(self.ops[e])
        for b in r:
            self._need(e, op, b.lw)
        for b in w:
            self._need(e, op, b.lw)
            for ev in b.readers.values():
                self._need(e, op, ev)
        self.ops[e].append(op)
        ev = ("e", e, idx)
        for b in w:
            b.lw = ev
            b.readers = {}
        for b in r:
            if b.lw is not ev:
                b.readers[e] = ev
        return ev

    def dma(self, q, out, in_, r=(), w=(), sem=None, **kw):
        name = self.dsem(sem)
        op = _Op(lambda eng: eng.dma_start(out=out, in_=in_, **kw))
        if self.semtot[name] > 0:
            self._need(q, op, ("d", name, self.semtot[name]))
        for b in r:
            self._need(q, op, b.lw)
        for b in w:
            self._need(q, op, b.lw)
            for ev in b.readers.values():
                self._need(q, op, ev)
        self.semtot[name] += 16
        op.dma = name
        self.ops[q].append(op)
        ev = ("d", name, self.semtot[name])
        for b in w:
            b.lw = ev
            b.readers = {}
        for b in r:
            b.readers["d_" + name] = ev
        return ev

    def wait_all(self, e, bufs):
        op = _Op(None)
        for b in bufs:
            self._need(e, op, b.lw)
            for ev in b.readers.values():
                self._need(e, op, ev)
        self.ops[e].append(op)

    def emit(self):
        nc = self.nc
        cum = {}
        self.maxcount = {}
        for e, lst in self.ops.items():
            c = 0
            arr = []
            for o in lst:
                if o.needs_inc:
                    c += 1
                arr.append(((c - 1) // SEM_BLK if c > 0 else 0, (c - 1) % SEM_BLK + 1 if c > 0 else 0))
            cum[e] = arr
            self.maxcount[e] = c
            for j in range((c + SEM_BLK - 1) // SEM_BLK + 1):
                nm = "e_%s_%d" % (e, j)
                self.sem[nm] = self.stack.enter_context(nc.semaphore(nm))
        sem = self.sem

        def run(e, handle):
            for i, o in enumerate(self.ops[e]):
                for ev in o.waits:
                    if ev[0] == "e":
                        blk, c = cum[ev[1]][ev[2]]
                        handle.wait_ge(sem["e_%s_%d" % (ev[1], blk)], c)
                    else:
                        handle.wait_ge(sem[ev[1]], ev[2])
                if o.fn is None:
                    continue
                ins = o.fn(handle)
                if o.dma is not None:
                    ins.then_inc(sem[o.dma], 16)
                elif o.needs_inc:
                    ins.then_inc(sem["e_%s_%d" % (e, cum[e][i][0])], 1)

        with nc.Block() as block:
            @block.sync
            def _(h):
                run("sp", h)

            @block.scalar
            def _(h):
                run("act", h)

            @block.vector
            def _(h):
                run("dve", h)

            @block.gpsimd
            def _(h):
                run("pool", h)

            @block.tensor
            def _(h):
                run("pe", h)


class Prog:
    def __init__(self, S, NSEQ, layer_ids, dbg=False, stop_after=None):
        self.stop_after = stop_after
        self.S = S
        self.NSEQ = NSEQ
        self.layer_ids = list(layer_ids)
        self.NT = S // T
        self.dbg = dbg
        self.nc = bass.Bass("TRN2", target_bir_lowering=False)
        with contextlib.ExitStack() as st:
            self.st = st
            self.sc = Sched(self.nc, st)
            self.declare()
            self.consts()
            self.body()
            self.sc.emit()

    def sb(self, name, shape, dt):
        return self.st.enter_context(self.nc.sbuf_tensor(name, shape, dt))

    def dram_in(self, name, shape, dt=F32):
        return self.nc.dram_tensor(name, list(shape), dt, kind="ExternalInput").ap()

    def dram_tmp(self, name, shape, dt):
        return self.nc.dram_tensor(name, list(shape), dt, kind="Internal").ap()

    def ACT(self, out, in_, func, r, w, **kw):
        self.sc.op("act", lambda e: e.activation(out=out, in_=in_, func=func, **kw), r=r, w=w)

    def MM(self, out, lhsT, rhs, start, stop, r, w):
        self.sc.op("pe", lambda e: e.matmul(out, lhsT=lhsT, rhs=rhs, start=start, stop=stop), r=r, w=w)

    def TR(self, out, in_, ident, r, w):
        self.sc.op("pe", lambda e: e.transpose(out, in_, ident), r=r, w=w)

    def TT(self, eng, out, in0, in1, op, r, w):
        self.sc.op(eng, lambda e: e.tensor_tensor(out=out, in0=in0, in1=in1, op=op), r=r, w=w)

    def TS(self, eng, out, in0, s1, s2, op0, op1, r, w):
        if s2 is None:
            self.sc.op(eng, lambda e: e.tensor_scalar(out=out, in0=in0, scalar1=s1, scalar2=None, op0=op0), r=r, w=w)
        else:
            self.sc.op(eng, lambda e: e.tensor_scalar(out=out, in0=in0, scalar1=s1, scalar2=s2, op0=op0, op1=op1), r=r, w=w)

    def STT(self, out, in0, scalar, in1, op0, op1, r, w, accum_out=None):
        if accum_out is None:
            self.sc.op("dve", lambda e: e.scalar_tensor_tensor(out=out, in0=in0, scalar=scalar, in1=in1, op0=op0, op1=op1), r=r, w=w)
        else:
            self.sc.op("dve", lambda e: e.scalar_tensor_tensor(out=out, in0=in0, scalar=scalar, in1=in1, op0=op0, op1=op1, accum_out=accum_out), r=r, w=w)

    def CP(self, eng, out, in_, r, w):
        if eng == "act":
            self.sc.op("act", lambda e: e.copy(out=out, in_=in_), r=r, w=w)
        else:
            self.sc.op(eng, lambda e: e.tensor_copy(out=out, in_=in_), r=r, w=w)

    def MEMSET(self, eng, ap, val, w):
        self.sc.op(eng, lambda e: e.memset(ap, val), w=w)

    def DMA(self, q, out, in_, r, w, sem, **kw):
        self.sc.dma(q, out, in_, r=r, w=w, sem=sem, **kw)

    def declare(self):
        S, NSEQ = self.S, self.NSEQ
        NTOK = S * NSEQ
        self.x_in = self.dram_in("x", [NTOK, D])
        self.cT = self.dram_in("cT", [128, KC, NSEQ])
        self.out = self.nc.dram_tensor("out", [NTOK, D], F32, kind="ExternalOutput").ap()
        self.B_stream = [[Buf("xs_%d_%d" % (s, t)) for t in range(self.NT)] for s in range(NSEQ)]
        self.W = {}
        for li in self.layer_ids:
            w = {}
            w["ada_w"] = self.dram_in("ada_w_%d" % li, [D, 6 * D])
            w["ada_bT"] = self.dram_in("ada_bT_%d" % li, [128, 6 * KC])
            w["normT"] = self.dram_in("normT_%d" % li, [128, 4 * KC])
            w["ffn_gu"] = self.dram_in("ffn_gu_%d" % li, [2 * FC, 128, KC * 128])
            w["ffn_dn"] = self.dram_in("ffn_dn_%d" % li, [4, 128, FC * 512])
            w["ffn_gu_b"] = self.dram_tmp("ffn_gu_b_%d" % li, [2 * FC, 128, KC * 128], BF16)
            w["ffn_dn_b"] = self.dram_tmp("ffn_dn_b_%d" % li, [4, 128, FC * 512], BF16)
            if li % 2 == 0:
                w["in"] = self.dram_in("hg_in_%d" % li, [64, 128, KC * 128])
                w["outw"] = self.dram_in("hg_out_%d" % li, [4, 128, 16 * 512])
                w["in_b"] = self.dram_tmp("hg_in_b_%d" % li, [64, 128, KC * 128], BF16)
                w["out_b"] = self.dram_tmp("hg_out_b_%d" % li, [4, 128, 16 * 512], BF16)
                w["lbT"] = self.dram_in("hg_lbT_%d" % li, [128, 2 * 16])
                w["hnw"] = self.dram_in("hg_nw_%d" % li, [128, 1])
            else:
                w["in"] = self.dram_in("gdn_in_%d" % li, [96, 128, KC * 128])
                w["ba"] = self.dram_in("gdn_ba_%d" % li, [128, KC * 64])
                w["outw"] = self.dram_in("gdn_out_%d" % li, [4, 128, 32 * 512])
                w["in_b"] = self.dram_tmp("gdn_in_b_%d" % li, [96, 128, KC * 128], BF16)
                w["ba_b"] = self.dram_tmp("gdn_ba_b_%d" % li, [128, KC * 64], BF16)
                w["out_b"] = self.dram_tmp("gdn_out_b_%d" % li, [4, 128, 32 * 512], BF16)
                w["convT"] = self.dram_in("gdn_convT_%d" % li, [128, 64 * 4])
                w["aldt"] = self.dram_in("gdn_aldt_%d" % li, [128, 64])
                w["gnw"] = self.dram_in("gdn_nw_%d" % li, [128, 1])
            w["B_cast"] = Buf("cast_%d" % li)
            self.W[li] = w
        nl = len(self.layer_ids)
        self.gvec = self.dram_tmp("gvec", [nl * 2 * NSEQ, D], F32)
        self.B_gvec = Buf("gvec")

        self.cst_in = self.dram_in("cst", [128, CST_COLS])
        self.ident_b = self.sb("ident_b", [128, 128], BF16)
        self.ones_b = self.sb("ones_b", [128, 128], BF16)
        self.ones1_b = self.sb("ones1_b", [128, 128], BF16)
        self.B_const = Buf("const")
        self.xblk = [self.sb("xblk%d" % i, [128, D], F32) for i in range(2)]
        self.B_xblk = [Buf("xblk%d" % i) for i in range(2)]
        self.gbc = self.sb("gbc", [128, D], F32)
        self.B_gbc = Buf("gbc")
        self.hT = self.sb("hT", [128, KC, T], BF16)
        self.B_hT = Buf("hT")
        self.hid = self.sb("hid", [128, FC, T], BF16)
        self.B_hid = [Buf("hid%d" % i) for i in range(FC)]
        self.state = self.sb("state", [128, 32, 128], F32)
        self.B_state = [Buf("state%d" % i) for i in range(32)]
        self.NU = 4
        self.uring = [self.sb("uring%d" % i, [128, KC, 128], BF16) for i in range(self.NU)]
        self.B_uring = [Buf("uring%d" % i) for i in range(self.NU)]
        self.u_next = 0
        self.pring = [self.sb("pring%d" % i, [128, 8, 512], BF16) for i in range(2)]
        self.B_pring = [Buf("pring%d" % i) for i in range(2)]
        self.p_next = 0
        self.NW = 18
        self.work = self.sb("work", [128, self.NW, 512], F32)
        self.B_work = [Buf("work%d" % i) for i in range(self.NW)]
        base = self.hid[:, 0:8, :].rearrange("p a b -> p (a b)").bitcast(F32)
        n1 = nl * 6 * KC * NSEQ
        n2 = nl * 6 * KC
        n3 = nl * 4 * KC
        self.modT = base[:, 0:n1].rearrange("p (l v k s) -> p l v k s", v=6, k=KC, s=NSEQ)
        self.adab = base[:, n1:n1 + n2].rearrange("p (l v k) -> p l v k", v=6, k=KC)
        self.normT = base[:, n1 + n2:n1 + n2 + n3].rearrange("p (l v k) -> p l v k", v=4, k=KC)
        self.B_modT = Buf("modT")
        self.AB = self.sb("AB", [128, nl, NSEQ, 4, KC], F32)
        self.B_AB = Buf("AB")
        self.cact = self.sb("cact", [128, KC, NSEQ], F32)
        self.B_cact = Buf("cact")
        self.small = self.sb("small", [128, 64], F32)
        self.B_small = [Buf("small%d" % i) for i in range(16)]
        self.B_par = Buf("par")
        self.ps = [self.st.enter_context(self.nc.psum_tensor("ps%d" % i, [128, 512], F32)) for i in range(8)]
        self.B_ps = [Buf("ps%d" % i) for i in range(8)]
        self.rr = 0

    def wslot(self, i, dt=F32):
        ap = self.work[:, i, :]
        return ap if dt == F32 else ap.bitcast(BF16)

    def evac_eng(self):
        self.rr ^= 1
        return "act" if self.rr else "dve"

    def consts(self):
        Bc = self.B_const
        self.cst = self.sb("cst_sb", [128, CST_COLS], F32)
        self.DMA("sp", self.cst[:], self.cst_in, [], [Bc], "ld_small")
        self.ident_f = self.cst[:, 0:128]
        self.CP("dve", self.ident_b[:], self.ident_f, [Bc], [Bc])
        self.MEMSET("pool", self.ones_b[:], 1.0 / 128.0, [Bc])
        self.MEMSET("pool", self.ones1_b[:], 1.0, [Bc])
        self.hg_mask = self.cst[:, 128:256]
        self.hg_reset = self.cst[:, 256:768]
        self.hg_vmask = self.cst[:, 768:1280].rearrange("p (n c) -> p n c", c=128)
        self.B_hgc = Bc

    def cast_layer(self, li):
        w = self.W[li]
        B = w["B_cast"]
        pairs = [("in", "in_b", 8), ("outw", "out_b", 2), ("ffn_gu", "ffn_gu_b", 8), ("ffn_dn", "ffn_dn_b", 4)]
        if li % 2 == 1:
            pairs.append(("ba", "ba_b", 1))
        i = 0
        for src, dst, nsplit in pairs:
            s_ap, d_ap = w[src], w[dst]
            n0 = s_ap.shape[0]
            if nsplit == 1:
                self.DMA("pool", d_ap, s_ap, [], [B], "cast%d_%d" % (li % 2, i))
                i += 1
                continue
            step = (n0 + nsplit - 1) // nsplit
            for a in range(0, n0, step):
                b = min(n0, a + step)
                self.DMA("pool", d_ap[a:b], s_ap[a:b], [], [B], "cast%d_%d" % (li % 2, i))
                i += 1

    def adaln(self):
        NSEQ = self.NSEQ
        nl = len(self.layer_ids)
        Bp = self.B_par
        self.DMA("sp", self.cact[:], self.cT, [], [self.B_cact], "ld_small")
        for k, li in enumerate(self.layer_ids):
            w = self.W[li]
            self.DMA("sp", self.normT[:, k].rearrange("p a b -> p (a b)"), w["normT"], [], [Bp], "ld_small")
            self.DMA("sp", self.adab[:, k].rearrange("p a b -> p (a b)"), w["ada_bT"], [], [Bp], "ld_small")
        self.ACT(self.cact[:], self.cact[:], AF.Silu, [self.B_cact], [self.B_cact])
        NP = 256
        panels = []
        for k, li in enumerate(self.layer_ids):
            for v in range(6):
                for c0 in range(0, D, NP):
                    panels.append((k, li, v, c0))
        pan_t = [self.work[:, 0:8, :].rearrange("p a b -> p (a b)").rearrange("p (k n) -> p k n", n=NP),
                 self.work[:, 8:16, :].rearrange("p a b -> p (a b)").rearrange("p (k n) -> p k n", n=NP)]
        pan_B = [self.B_work[0:8], self.B_work[8:16]]
        bank = self.ps[0]
        for i, (k, li, v, c0) in enumerate(panels):
            sl = i % 2
            src = self.W[li]["ada_w"][:, v * D + c0: v * D + c0 + NP].rearrange("(kc p) n -> p kc n", p=128)
            self.DMA("sp", pan_t[sl], src, [], pan_B[sl], "ld_ada%d" % sl)
            for j in range(NP // 128):
                col = (c0 // 128 + j)
                o = bank[:, col * NSEQ:(col + 1) * NSEQ]
                for kc in range(KC):
                    self.MM(o, pan_t[sl][:, kc, j * 128:(j + 1) * 128], self.cact[:, kc, :], kc == 0, kc == KC - 1,
                            pan_B[sl] + [self.B_cact], [self.B_ps[0]])
            if c0 + NP == D:
                self.TT("dve", self.modT[:, k, v], bank[:, 0:KC * NSEQ].rearrange("p (a s) -> p a s", s=NSEQ),
                        self.adab[:, k, v].unsqueeze(2).to_broadcast([128, KC, NSEQ]), ALU.add,
                        [Bp], [self.B_ps[0], self.B_modT])
        for k, li in enumerate(self.layer_ids):
            for s in range(NSEQ):
                for (dst, nrm, v_scale, v_shift) in ((0, 0, 1, 0), (2, 2, 4, 3)):
                    self.STT(self.AB[:, k, s, dst], self.modT[:, k, v_scale, :, s], 1.0, self.normT[:, k, nrm],
                             ALU.add, ALU.mult, [self.B_modT, Bp], [self.B_AB])
                    self.CP("dve", self.AB[:, k, s, dst + 1], self.modT[:, k, v_shift, :, s], [self.B_modT], [self.B_AB])
                for sub, (v_gate, nrm) in enumerate(((2, 1), (5, 3))):
                    g = self.small[:, 0:KC]
                    self.TT("dve", g, self.modT[:, k, v_gate, :, s], self.normT[:, k, nrm], ALU.mult,
                            [self.B_modT, Bp], [self.B_small[0]])
                    self.MM(self.ps[1][0:KC, 0:128], g, self.ident_f, True, True,
                            [self.B_small[0], self.B_const], [self.B_ps[1]])
                    stg = self.work[0:KC, 16, 0:128]
                    self.CP("dve", stg, self.ps[1][0:KC, 0:128], [], [self.B_ps[1], self.B_work[16]])
                    row = (k * 2 + sub) * NSEQ + s
                    self.DMA("pool", self.gvec[row].rearrange("(kc p) -> kc p", p=128), stg,
                             [self.B_work[16]], [self.B_gvec], "st_small")

    def load_unit(self, dram_units, u):
        sl = self.u_next
        self.u_next = (self.u_next + 1) % self.NU
        self.DMA("sp", self.uring[sl][:].rearrange("p a b -> p (a b)"), dram_units[u],
                 [self.cur_cast], [self.B_uring[sl]], "ld_u%d" % sl)
        return self.uring[sl], self.B_uring[sl]

    def load_panel(self, dram_panels, dq, kc0, nk, KCt):
        sl = self.p_next
        self.p_next = (self.p_next + 1) % 2
        src = dram_panels[dq].rearrange("p (k c) -> p k c", c=512)[:, kc0:kc0 + nk, :]
        self.DMA("sp", self.pring[sl][:, 0:nk, :], src, [self.cur_cast], [self.B_pring[sl]], "ld_p%d" % sl)
        return self.pring[sl], self.B_pring[sl]

    def stream_rows(self, s, t, blk):
        r0 = s * self.S + t * T + blk * 128
        return slice(r0, r0 + 128)

    def prologue_block(self, k, s, blk, slot, which):
        xb, Bx = self.xblk[slot], self.B_xblk[slot]
        ws = [self.B_work[4 * blk + i] for i in range(4)]
        junk = self.work[:, 4 * blk:4 * blk + 4, :].rearrange("p a b -> p (a b)")
        ssq = self.small[:, 16 + blk:17 + blk]
        Bs = self.B_small[1 + blk]
        self.ACT(junk, xb[:], AF.Square, [Bx], ws + [Bs], accum_out=ssq)
        self.ACT(ssq, ssq, AF.Sqrt, [], [Bs], scale=1.0 / D, bias=EPS)
        self.sc.op("dve", lambda e: e.reciprocal(out=ssq, in_=ssq), w=[Bs])
        self.TS("dve", junk, xb[:], ssq, None, ALU.mult, None, [Bx, Bs], ws)
        pb = 2 + (blk % 2)
        for g4 in range(4):
            for j in range(4):
                kc = g4 * 4 + j
                self.TR(self.ps[pb][:, j * 128:(j + 1) * 128], junk[:, kc * 128:(kc + 1) * 128], self.ident_f,
                        ws + [self.B_const], [self.B_ps[pb]])
            for j in range(4):
                kc = g4 * 4 + j
                o = self.hT[:, kc, blk * 128:(blk + 1) * 128]
                i = self.ps[pb][:, j * 128:(j + 1) * 128]
                A = self.AB[:, k, s, which, kc:kc + 1]
                Bv = self.AB[:, k, s, which + 1, kc:kc + 1]
                if (kc % 2) == 0:
                    self.ACT(o, i, AF.Identity, [self.B_AB], [self.B_ps[pb], self.B_hT], scale=A, bias=Bv)
                else:
                    self.TS("dve", o, i, A, Bv, ALU.mult, ALU.add, [self.B_AB], [self.B_ps[pb], self.B_hT])

    def prologue_tile(self, k, s, t, src):
        for blk in range(NBLK):
            slot = blk % 2
            self.DMA("sp", self.xblk[slot][:], src[self.stream_rows(s, t, blk), :],
                     [self.B_stream[s][t]], [self.B_xblk[slot]], "ld_x%d" % slot)
            self.prologue_block(k, s, blk, slot, 0)

    def epilogue_tile(self, k, s, t, sub, panels, KCt, src, next_prologue):
        grow = (k * 2 + sub) * self.NSEQ + s
        self.DMA("sp", self.gbc[:], self.gvec[grow:grow + 1, :].to_broadcast([128, D]),
                 [self.B_gvec], [self.B_gbc], "ld_g")
        ssqp = self.small[:, 32:48].rearrange("p (b q) -> p b q", q=4)
        Bq = self.B_small[6]
        step = 8
        for dq in range(4):
            for kc0 in range(0, KCt, step):
                nk = min(step, KCt - kc0)
                pt, Bp = self.load_panel(panels, dq, kc0, nk, KCt)
                for blk in range(NBLK):
                    for j in range(nk):
                        kc = kc0 + j
                        self.MM(self.ps[4 + blk][:], self.hid[:, kc, blk * 128:(blk + 1) * 128], pt[:, j, :],
                                kc == 0, kc == KCt - 1, [self.B_hid[kc], Bp], [self.B_ps[4 + blk]])
            for blk in range(NBLK):
                y = self.work[:, 4 * blk + dq, :]
                By = self.B_work[4 * blk + dq]
                self.ACT(y, self.ps[4 + blk][:], AF.Copy, [], [self.B_ps[4 + blk], By])
                self.STT(self.wslot(16, BF16)[:, 0:512], y, 1.0, y, ALU.mult, ALU.mult, [By], [self.B_work[16], Bq],
                         accum_out=ssqp[:, blk, dq:dq + 1])
        for blk in range(NBLK):
            slot = blk % 2
            xb, Bx = self.xblk[slot], self.B_xblk[slot]
            ws = [self.B_work[4 * blk + i] for i in range(4)]
            yb = self.work[:, 4 * blk:4 * blk + 4, :].rearrange("p a b -> p (a b)")
            self.DMA("sp", xb[:], src[self.stream_rows(s, t, blk), :], [self.B_stream[s][t]], [Bx], "ld_x%d" % slot)
            rs = self.small[:, 20 + blk:21 + blk]
            Br = self.B_small[7 + blk]
            self.sc.op("dve", lambda e, rs=rs, blk=blk: e.tensor_reduce(out=rs, in_=ssqp[:, blk, :], axis=mybir.AxisListType.X, op=ALU.add),
                       r=[Bq], w=[Br])
            self.ACT(rs, rs, AF.Sqrt, [], [Br], scale=1.0 / D, bias=EPS)
            self.sc.op("dve", lambda e, rs=rs: e.reciprocal(out=rs, in_=rs), w=[Br])
            self.STT(yb, yb, rs, self.gbc[:], ALU.mult, ALU.mult, [Br, self.B_gbc], ws)
            self.TT("pool", xb[:], xb[:], yb, ALU.add, ws, [Bx])
            self.DMA("pool", self.out[self.stream_rows(s, t, blk), :], xb[:], [Bx], [self.B_stream[s][t]], "st_x%d" % slot)
            if next_prologue:
                self.prologue_block(k, s, blk, slot, 2)

    def ffn_tile(self, k, s, t):
        li = self.layer_ids[k]
        w = self.W[li]
        for m in range(FC):
            gt, Bg = self.load_unit(w["ffn_gu_b"], 2 * m)
            ut, Bu = self.load_unit(w["ffn_gu_b"], 2 * m + 1)
            pg, pu = (m % 2), 2 + (m % 2)
            for kc in range(KC):
                self.MM(self.ps[pg][:], gt[:, kc, :], self.hT[:, kc, :], kc == 0, kc == KC - 1,
                        [Bg, self.B_hT], [self.B_ps[pg]])
            for kc in range(KC):
                self.MM(self.ps[pu][:], ut[:, kc, :], self.hT[:, kc, :], kc == 0, kc == KC - 1,
                        [Bu, self.B_hT], [self.B_ps[pu]])
            tmp = self.work[:, 16 + (m % 2), :]
            Bt = self.B_work[16 + (m % 2)]
            self.ACT(tmp, self.ps[pg][:], AF.Silu, [], [self.B_ps[pg], Bt])
            self.TT("dve", self.hid[:, m, :], self.ps[pu][:], tmp, ALU.mult, [Bt], [self.B_ps[pu], self.B_hid[m]])

    def hg_setup(self, k):
        li = self.layer_ids[k]
        w = self.W[li]
        Bp = self.B_par
        if not hasattr(self, "hg_par"):
            self.hg_par = self.sb("hg_par", [128, 4, 16], F32)
            self.hg_lg = self.sb("hg_lg", [128, 32], F32)
            self.hg_nw = self.sb("hg_nw", [128, 1], F32)
        self.DMA("sp", self.hg_lg[:], w["lbT"], [], [Bp], "ld_small")
        self.DMA("sp", self.hg_nw[:], w["hnw"], [], [Bp], "ld_small")
        lb, oml, noml = self.hg_par[:, 0], self.hg_par[:, 1], self.hg_par[:, 2]
        if li == 0:
            self.MEMSET("dve", lb, 0.0, [Bp])
        else:
            self.TT("dve", lb, self.hg_lg[:, 16:32], self.hg_lg[:, 0:16], ALU.subtract, [], [Bp])
            self.ACT(lb, lb, AF.Sigmoid, [], [Bp])
        self.TS("dve", oml, lb, -1.0, 1.0, ALU.mult, ALU.add, [], [Bp])
        self.TS("dve", noml, oml, -1.0, None, ALU.mult, None, [], [Bp])

    def hg_tile(self, k, s, t):
        li = self.layer_ids[k]
        w = self.W[li]
        units = w["in_b"]
        Bp, Bc = self.B_par, self.B_hgc
        W_ = self.work
        BW = self.B_work
        QS, SIG, G, KF, B_, E1, E2, DIF, GS, OSB, RS = 0, 1, 2, 3, 4, 5, 6, 7, 8, 9, 10
        QT, KT, KDT, KDV, VBM0, VBM1, MISC = 11, 12, 13, 14, 15, 16, 17
        qt = self.wslot(QT, BF16)[:, 0:T]
        kt = self.wslot(KT, BF16)[:, 0:T]
        kdT = self.wslot(KDT, BF16)[:, 0:T]
        kd_tok = self.wslot(KDV, BF16)[:, 0:512].rearrange("p (b c) -> p b c", c=128)
        v_tok = self.wslot(KDV, BF16)[:, 512:1024].rearrange("p (b c) -> p b c", c=128)
        vbm = self.work[:, VBM0:VBM1 + 1, :].rearrange("p a b -> p (a b)").bitcast(BF16).rearrange("p (b n c) -> p b n c", n=4, c=128)
        Bvbm = [BW[VBM0], BW[VBM1]]
        misc = self.wslot(MISC, BF16)
        scm = [misc[:, 0:128], misc[:, 128:256]]
        sbf = [misc[:, 256:384], misc[:, 384:512]]
        osq = misc[:, 512:1024]
        dn = self.small[:, 48:64]
        Bdn = self.B_small[12]
        for h in range(16):
            lbh, omlh, nomlh = self.hg_par[:, 0, h:h + 1], self.hg_par[:, 1, h:h + 1], self.hg_par[:, 2, h:h + 1]
            S_f = self.state[:, h, :]
            BS = self.B_state[h]
            if t == 0:
                self.MEMSET("pool", S_f, 0.0, [BS])
            ut, Bu = self.load_unit(units, 4 * h + 0)
            pq = 0
            for kc in range(KC):
                self.MM(self.ps[pq][:], ut[:, kc, :], self.hT[:, kc, :], kc == 0, kc == KC - 1, [Bu, self.B_hT], [self.B_ps[pq]])
            self.ACT(W_[:, QS, :], self.ps[pq][:], AF.Silu, [], [self.B_ps[pq], BW[QS]])
            ut, Bu = self.load_unit(units, 4 * h + 1)
            pf = 1
            for kc in range(KC):
                self.MM(self.ps[pf][:], ut[:, kc, :], self.hT[:, kc, :], kc == 0, kc == KC - 1, [Bu, self.B_hT], [self.B_ps[pf]])
            self.ACT(W_[:, SIG, :], self.ps[pf][:], AF.Sigmoid, [], [self.B_ps[pf], BW[SIG]])
            self.ACT(W_[:, G, :], W_[:, SIG, :], AF.Ln, [BW[SIG], Bp], [BW[G]], scale=omlh, bias=lbh)
            self.TS("dve", W_[:, KF, :], W_[:, SIG, :], nomlh, omlh, ALU.mult, ALU.add, [BW[SIG], Bp], [BW[KF]])
            self.sc.op("dve", lambda e: e.tensor_tensor_scan(out=W_[:, B_, :], data0=self.hg_reset, data1=W_[:, G, :],
                                                             initial=0.0, op0=ALU.mult, op1=ALU.add),
                       r=[Bc, BW[G]], w=[BW[B_]])
            self.ACT(W_[:, E1, :], W_[:, B_, :], AF.Exp, [BW[B_]], [BW[E1]])
            self.TT("dve", qt, W_[:, QS, :], W_[:, E1, :], ALU.mult, [BW[QS], BW[E1]], [BW[QT]])
            self.ACT(W_[:, E2, :], W_[:, B_, :], AF.Exp, [BW[B_]], [BW[E2]], scale=-1.0)
            self.TT("dve", kt, W_[:, KF, :], W_[:, E2, :], ALU.mult, [BW[KF], BW[E2]], [BW[KT]])
            b3 = W_[:, B_, :].rearrange("p (c k) -> p c k", k=32)
            self.TT("dve", W_[:, DIF, :].rearrange("p (c k) -> p c k", k=32), b3[:, :, 31:32].to_broadcast([128, 16, 32]), b3,
                    ALU.subtract, [BW[B_]], [BW[DIF]])
            self.ACT(W_[:, DIF, :], W_[:, DIF, :], AF.Exp, [], [BW[DIF]])
            self.TT("dve", kdT, W_[:, KF, :], W_[:, DIF, :], ALU.mult, [BW[KF], BW[DIF]], [BW[KDT]])
            self.ACT(dn, b3[:, :, 31], AF.Exp, [BW[B_]], [Bdn])
            pm = 2
            pmb = self.ps[pm][:].bitcast(BF16)
            for blk in range(NBLK):
                self.TR(pmb[:, blk * 128:(blk + 1) * 128], kdT[:, blk * 128:(blk + 1) * 128], self.ident_b[:],
                        [BW[KDT], self.B_const], [self.B_ps[pm]])
            self.CP("act", kd_tok.rearrange("p b c -> p (b c)"), pmb[:, 0:512], [], [self.B_ps[pm], BW[KDV]])
            ut, Bu = self.load_unit(units, 4 * h + 2)
            pv = 0
            for blk in range(NBLK):
                for kc in range(KC):
                    self.MM(self.ps[pv][:, blk * 128:(blk + 1) * 128], self.hT[:, kc, blk * 128:(blk + 1) * 128], ut[:, kc, :],
                            kc == 0, kc == KC - 1, [Bu, self.B_hT], [self.B_ps[pv]])
            self.CP("act", v_tok.rearrange("p b c -> p (b c)"), self.ps[pv][:], [], [self.B_ps[pv], BW[KDV]])
            for blk in range(NBLK):
                self.TT("dve", vbm[:, blk], self.ps[pv][:, blk * 128:(blk + 1) * 128].unsqueeze(1).to_broadcast([128, 4, 128]),
                        self.hg_vmask, ALU.mult, [Bc], [self.B_ps[pv], Bvbm[blk // 2]])
            ut, Bu = self.load_unit(units, 4 * h + 3)
            pg = 1
            for kc in range(KC):
                self.MM(self.ps[pg][:], ut[:, kc, :], self.hT[:, kc, :], kc == 0, kc == KC - 1, [Bu, self.B_hT], [self.B_ps[pg]])
            self.ACT(W_[:, GS, :], self.ps[pg][:], AF.Sigmoid, [], [self.B_ps[pg], BW[GS]])
            po, pu_ = 4, 3
            self.CP("act", sbf[0], S_f, [BS], [BW[MISC]])
            cur = 0
            for blk in range(NBLK):
                c0 = blk * 128
                self.MM(self.ps[pm][:, 0:128], kt[:, c0:c0 + 128], qt[:, c0:c0 + 128], True, True, [BW[KT], BW[QT]], [self.B_ps[pm]])
                self.TT("dve", scm[blk % 2], self.ps[pm][:, 0:128], self.hg_mask, ALU.mult, [Bc], [self.B_ps[pm], BW[MISC]])
                self.MM(self.ps[pu_][:], kd_tok[:, blk, :], vbm[:, blk].rearrange("p n c -> p (n c)"), True, True,
                        [BW[KDV], Bvbm[blk // 2]], [self.B_ps[pu_]])
                self.MM(self.ps[po][:, c0:c0 + 128], v_tok[:, blk, :], scm[blk % 2], True, False, [BW[KDV], BW[MISC]], [self.B_ps[po]])
                for n in range(4):
                    c = blk * 4 + n
                    self.MM(self.ps[po][:, c0 + 32 * n:c0 + 32 * n + 32], sbf[cur], qt[:, c0 + 32 * n:c0 + 32 * n + 32],
                            False, n == 3, [BW[MISC], BW[QT]], [self.B_ps[po]])
                    self.STT(S_f, S_f, dn[:, c:c + 1], self.ps[pu_][:, n * 128:(n + 1) * 128], ALU.mult, ALU.add,
                             [Bdn], [self.B_ps[pu_], BS])
                    cur ^= 1
                    self.CP("act", sbf[cur], S_f, [BS], [BW[MISC]])
            self.ACT(osq, self.ps[po][:], AF.Square, [], [self.B_ps[po], BW[MISC]])
            self.CP("dve", W_[:, OSB, :], self.ps[po][:], [], [self.B_ps[po], BW[OSB]])
            self.MM(self.ps[pm][:], self.ones_b[:], osq, True, True, [self.B_const, BW[MISC]], [self.B_ps[pm]])
            self.ACT(W_[:, RS, :], self.ps[pm][:], AF.Sqrt, [], [self.B_ps[pm], BW[RS]], bias=EPS)
            self.sc.op("dve", lambda e: e.reciprocal(out=W_[:, RS, :], in_=W_[:, RS, :]), w=[BW[RS]])
            self.STT(W_[:, OSB, :], W_[:, OSB, :], self.hg_nw[:, 0:1], W_[:, RS, :], ALU.mult, ALU.mult, [Bp, BW[RS]], [BW[OSB]])
            self.TT("dve", self.hid[:, h, :], W_[:, OSB, :], W_[:, GS, :], ALU.mult, [BW[OSB], BW[GS]], [self.B_hid[h]])

    def body(self):
        NSEQ = self.NSEQ
        nl = len(self.layer_ids)
        for li in self.layer_ids[:1]:
            self.cast_layer(li)
        self.adaln()
        for k, li in enumerate(self.layer_ids):
            w = self.W[li]
            if k + 1 < nl:
                self.cast_layer(self.layer_ids[k + 1])
            self.cur_cast = w["B_cast"]
            if li % 2 == 0:
                self.hg_setup(k)
            else:
                self.gdn_setup(k)
            for s in range(NSEQ):
                for t in range(self.NT):
                    src = self.x_in if k == 0 else self.out
                    self.prologue_tile(k, s, t, src)
                    if li % 2 == 0:
                        self.hg_tile(k, s, t)
                        KCt = 16
                    else:
                        self.gdn_tile(k, s, t)
                        KCt = 32
                    last = (k == nl - 1) and self.stop_after == "mixer"
                    self.epilogue_tile(k, s, t, 0, w["out_b"], KCt, src, not last)
                    if last:
                        continue
                    self.ffn_tile(k, s, t)
                    self.epilogue_tile(k, s, t, 1, w["ffn_dn_b"], FC, self.out, False)
        allb = [b for row in self.B_stream for b in row]
        self.sc.wait_all("pool", allb)
        self.sc.wait_all("sp", allb)

    def gdn_setup(self, k):
        li = self.layer_ids[k]
        w = self.W[li]
        Bp = self.B_par
        if not hasattr(self, "g_conv"):
            self.g_conv = self.sb("g_conv", [128, 64, 4], F32)
            self.g_aldt = self.sb("g_aldt", [128, 64], F32)
            self.g_negA = self.sb("g_negA", [128, 32], F32)
            self.g_nw = self.sb("g_nw", [128, 1], F32)
            self.g_ba = self.sb("g_ba", [128, KC, 64], BF16)
            self.g_carry = self.sb("g_carry", [128, 64, 3], F32)
            self.B_carry = [Buf("carry%d" % i) for i in range(64)]
            self.gsm = self.sb("gsm", [128, 6, 4, 32], F32)
            self.B_gsm = Buf("gsm")
            self.gw_f2 = [self.sb("gw_f%d" % h, [128, 12, 128], F32) for h in range(2)]
            self.B_gwf2 = [[Buf("gwf%d_%d" % (h, i)) for i in range(14)] for h in range(2)]
            self.gw_b2 = [self.sb("gw_b%d" % h, [128, 10, 128], BF16) for h in range(2)]
            self.B_gwb2 = [[Buf("gwb%d_%d" % (h, i)) for i in range(16)] for h in range(2)]
            self.B_gba = Buf("gba")
            self.kkqk_sb = self.sb("kkqk_sb", [128, 4, 256], F32)
            self.B_kkqk = [Buf("kkqk%d" % i) for i in range(4)]
            self.g_tri = self.cst[:, 1280:1408]
            self.g_blk1 = self.cst[:, 1408:1536]
            self.g_neg = self.cst[:, 1536:1664]
            self.g_pos = self.cst[:, 1664:1792]
            self.ones_f = self.cst[:, 1792:1920]
            self.pm_rot = 0
            self.ident_r = self.sb("ident_r", [128, 128], F32R if GDN_F32R else F32)
            self.CP("dve", self.ident_r[:], self.ident_f, [self.B_const], [self.B_const])
        self.DMA("sp", self.g_conv[:].rearrange("p a b -> p (a b)"), w["convT"], [], [Bp], "ld_small")
        self.DMA("sp", self.g_aldt[:], w["aldt"], [], [Bp], "ld_small")
        self.DMA("sp", self.g_nw[:], w["gnw"], [], [Bp], "ld_small")
        self.DMA("sp", self.g_ba[:].rearrange("p a b -> p (a b)"), w["ba_b"], [w["B_cast"]], [self.B_gba], "ld_small")
        self.ACT(self.g_negA[:], self.g_aldt[:, 0:32], AF.Exp, [], [Bp])
        self.TS("dve", self.g_negA[:], self.g_negA[:], -1.0, None, ALU.mult, None, [], [Bp])

    def pm(self):
        banks = (2, 3, 6, 7)
        self.pm_rot = (self.pm_rot + 1) % len(banks)
        b = banks[self.pm_rot]
        return self.ps[b], self.B_ps[b]

    def gdn_tile(self, k, s, t):
        li = self.layer_ids[k]
        w = self.W[li]
        units = w["in_b"]
        Bp, Bc = self.B_par, self.B_const
        W_, BW = self.work, self.B_work
        NLEV = GDN_NLEV
        XP0, XP1, ACC, RINV, SQ, QK, KV, ZS, VT, OSB, RS, OSQ = 0, 1, 2, 3, 4, 5, 6, 7, 8, 9, 10, 11
        xpad = self.work[:, XP0:XP1 + 1, :].rearrange("p a b -> p (a b)")[:, 0:T + 3]
        Bxp = [BW[XP0], BW[XP1]]
        acc = W_[:, ACC, :]
        sq = self.wslot(SQ, BF16)[:, 0:T]
        qT = self.wslot(QK, BF16)[:, 0:T]
        kT = self.wslot(QK, BF16)[:, T:2 * T]
        k_tok = self.wslot(KV, BF16)[:, 0:512].rearrange("p (b c) -> p b c", c=128)
        v_tok = self.wslot(KV, BF16)[:, 512:1024].rearrange("p (b c) -> p b c", c=128)
        zs = self.wslot(ZS, BF16)[:, 0:T]
        vT = self.wslot(VT, BF16)[:, 0:T]
        osq = self.wslot(OSQ, BF16)[:, 0:T]
        gsm = self.gsm
        BET, GTK, GCS, TOT, BK, EKD = 0, 1, 2, 3, 4, 5
        Bg = self.B_gsm
        DIAG, E1, E2, EGC, USB = range(5)
        ATT, TT_A, VB, KBG, KD, WT, QG, VN, SB0, SB1 = range(10)

        if t == 0:
            self.MEMSET("pool", self.g_carry[:], 0.0, self.B_carry)
            for vh in range(32):
                self.MEMSET("pool", self.state[:, vh, :], 0.0, [self.B_state[vh]])

        pb, Bpb = self.ps[0], self.B_ps[0]
        for blk in range(NBLK):
            for kc in range(KC):
                self.MM(pb[:, blk * 64:(blk + 1) * 64], self.hT[:, kc, blk * 128:(blk + 1) * 128], self.g_ba[:, kc, :],
                        kc == 0, kc == KC - 1, [self.B_hT, self.B_gba], [Bpb])
        pb3 = pb[:, 0:256].rearrange("p (b c) -> p b c", c=64)
        self.ACT(gsm[:, BET], pb3[:, :, 0:32], AF.Sigmoid, [], [Bpb, Bg])
        self.TT("dve", gsm[:, GTK], pb3[:, :, 32:64], self.g_aldt[:, 32:64].unsqueeze(1).to_broadcast([128, 4, 32]), ALU.add,
                [Bp], [Bpb, Bg])
        self.ACT(gsm[:, GTK], gsm[:, GTK], AF.Exp, [], [Bg])
        self.ACT(gsm[:, GTK], gsm[:, GTK], AF.Ln, [], [Bg], bias=1.0)
        self.TT("dve", gsm[:, GTK], gsm[:, GTK], self.g_negA[:].unsqueeze(1).to_broadcast([128, 4, 32]), ALU.mult, [Bp], [Bg])
        pc, Bpc = self.ps[1], self.B_ps[1]
        for blk in range(NBLK):
            self.MM(pc[:, blk * 32:(blk + 1) * 32], self.g_tri, gsm[:, GTK, blk, :], True, True, [Bc, Bg], [Bpc])
            self.MM(pc[:, 128 + blk * 32:128 + (blk + 1) * 32], self.g_blk1, gsm[:, GTK, blk, :], True, True, [Bc, Bg], [Bpc])
        self.CP("dve", gsm[:, GCS].rearrange("p b c -> p (b c)"), pc[:, 0:128], [], [Bpc, Bg])
        self.CP("dve", gsm[:, TOT].rearrange("p b c -> p (b c)"), pc[:, 128:256], [], [Bpc, Bg])
        self.ACT(gsm[:, BK], gsm[:, GCS], AF.Exp, [], [Bg])
        self.TT("dve", gsm[:, BK], gsm[:, BK], gsm[:, BET], ALU.mult, [], [Bg])
        self.TT("dve", gsm[:, EKD], gsm[:, TOT], gsm[:, GCS], ALU.subtract, [], [Bg])
        self.ACT(gsm[:, EKD], gsm[:, EKD], AF.Exp, [], [Bg])

        def proj_fm(u, pbank):
            ut, Bu = self.load_unit(units, u)
            for kc in range(KC):
                self.MM(self.ps[pbank][:], ut[:, kc, :], self.hT[:, kc, :], kc == 0, kc == KC - 1, [Bu, self.B_hT], [self.B_ps[pbank]])

        def conv_silu(pbank, cc, out_ap, out_bufs):
            Bcar = self.B_carry[cc]
            self.CP("dve", xpad[:, 0:3], self.g_carry[:, cc, :], [Bcar], Bxp)
            self.ACT(xpad[:, 3:T + 3], self.ps[pbank][:], AF.Copy, [], [self.B_ps[pbank]] + Bxp)
            self.CP("dve", self.g_carry[:, cc, :], xpad[:, T:T + 3], Bxp, [Bcar])
            cw = self.g_conv[:, cc, :]
            self.TS("dve", acc, xpad[:, 3:T + 3], cw[:, 3:4], None, ALU.mult, None, Bxp + [Bp], [BW[ACC]])
            for j in (2, 1, 0):
                self.STT(acc, xpad[:, j:j + T], cw[:, j:j + 1], acc, ALU.mult, ALU.add, Bxp + [Bp], [BW[ACC]])
            self.ACT(out_ap, acc, AF.Silu, [BW[ACC]], out_bufs)

        def l2norm(dst, scale):
            self.ACT(sq, acc, AF.Square, [BW[ACC]], [BW[SQ]])
            pt, Bpt = self.pm()
            self.MM(pt[:], self.ones1_b[:], sq, True, True, [Bc, BW[SQ]], [Bpt])
            self.ACT(W_[:, RINV, :], pt[:], AF.Sqrt, [], [Bpt, BW[RINV]], bias=EPS)
            self.sc.op("dve", lambda e: e.reciprocal(out=W_[:, RINV, :], in_=W_[:, RINV, :]), w=[BW[RINV]])
            self.STT(dst, acc, scale, W_[:, RINV, :], ALU.mult, ALU.mult, [BW[ACC], BW[RINV]], [BW[QK]])

        def to_tok(srcT, dst, dst_bufs, src_bufs):
            pt, Bpt = self.pm()
            ptb = pt[:].bitcast(BF16)
            for blk in range(NBLK):
                self.TR(ptb[:, blk * 128:(blk + 1) * 128], srcT[:, blk * 128:(blk + 1) * 128], self.ident_b[:], src_bufs + [Bc], [Bpt])
            self.CP("act", dst.rearrange("p b c -> p (b c)"), ptb[:, 0:512], [], [Bpt] + dst_bufs)

        for g in range(16):
            proj_fm(6 * g + 0, 0)
            conv_silu(0, g, acc, [BW[ACC]])
            l2norm(qT, 128.0 ** -0.5)
            proj_fm(6 * g + 1, 1)
            conv_silu(1, 16 + g, acc, [BW[ACC]])
            l2norm(kT, 1.0)
            to_tok(kT, k_tok, [BW[KV]], [BW[QK]])
            for blk in range(NBLK):
                c0 = blk * 128
                pt, Bpt = self.pm()
                self.MM(pt[:, 0:128], kT[:, c0:c0 + 128], kT[:, c0:c0 + 128], True, True, [BW[QK]], [Bpt])
                self.MM(pt[:, 128:256], kT[:, c0:c0 + 128], qT[:, c0:c0 + 128], True, True, [BW[QK]], [Bpt])
                self.CP("act", self.kkqk_sb[:, blk, :], pt[:, 0:256], [], [Bpt, self.B_kkqk[blk]])
            VTOK = [(KV, 512), (12, 0)]
            ZSL = [ZS, 13]
            for hh in range(2):
                vh = 2 * g + hh
                vtk = self.wslot(VTOK[hh][0], BF16)[:, VTOK[hh][1]:VTOK[hh][1] + 512].rearrange("p (b c) -> p b c", c=128)
                proj_fm(6 * g + 2 + 2 * hh, 0)
                conv_silu(0, 32 + vh, vT, [BW[VT]])
                to_tok(vT, vtk, [BW[VTOK[hh][0]]], [BW[VT]])
                proj_fm(6 * g + 3 + 2 * hh, 1)
                self.ACT(self.wslot(ZSL[hh], BF16)[:, 0:T], self.ps[1][:], AF.Silu, [], [self.B_ps[1], BW[ZSL[hh]]])

            rr = (lambda ap: ap.bitcast(F32R)) if GDN_F32R else (lambda ap: ap)

            def head_chain(hh):
                vh = 2 * g + hh
                S_f = self.state[:, vh, :]
                BS = self.B_state[vh]
                gw_f, gw_b = self.gw_f2[hh], self.gw_b2[hh]
                Bf, Bb = self.B_gwf2[hh], self.B_gwb2[hh]
                gwf = lambda i: gw_f[:, i, :]
                gwb = lambda i: gw_b[:, i, :]
                v_tok = self.wslot(VTOK[hh][0], BF16)[:, VTOK[hh][1]:VTOK[hh][1] + 512].rearrange("p (b c) -> p b c", c=128)
                BVT = BW[VTOK[hh][0]]
                zs_h = self.wslot(ZSL[hh], BF16)[:, 0:T]
                BZ = BW[ZSL[hh]]
                OSB_h, RS_h, OSQ_h = (OSB, RS, OSQ) if hh == 0 else (15, 16, 17)
                osq_h = self.wslot(OSQ_h, BF16)[:, 0:T]
                po, Bpo = self.ps[4 + hh], self.B_ps[4 + hh]
                self.CP("act", gwb(SB0), S_f, [BS], [Bb[SB0]])
                cur = SB0
                yield
                for blk in range(NBLK):
                    c0 = blk * 128
                    kk_sb = self.kkqk_sb[:, blk, 0:128]
                    qk_sb = self.kkqk_sb[:, blk, 128:256]
                    Bkq = self.B_kkqk[blk]
                    gcs = gsm[:, GCS, blk, vh:vh + 1]
                    bet = gsm[:, BET, blk, vh:vh + 1]
                    bk = gsm[:, BK, blk, vh:vh + 1]
                    ekd = gsm[:, EKD, blk, vh:vh + 1]
                    self.ACT(gwf(DIAG), self.ones_f, AF.Copy, [Bc, Bg], [Bf[DIAG]], scale=gcs)
                    self.ACT(gwb(VB), v_tok[:, blk, :], AF.Copy, [BVT, Bg], [Bb[VB]], scale=bet)
                    self.ACT(gwb(KBG), k_tok[:, blk, :], AF.Copy, [BW[KV], Bg], [Bb[KBG]], scale=bk)
                    self.ACT(gwb(KD), k_tok[:, blk, :], AF.Copy, [BW[KV], Bg], [Bb[KD]], scale=ekd)
                    yield
                    pg, Bpg = self.pm()
                    self.TR(pg[:, 0:128], gwf(DIAG), self.ident_f, [Bc, Bf[DIAG]], [Bpg])
                    yield
                    self.STT(gwf(E2), pg[:, 0:128], gcs, self.g_pos, ALU.subtract, ALU.add, [Bg, Bc], [Bpg, Bf[E2]])
                    self.STT(gwf(E1), pg[:, 0:128], gcs, self.g_neg, ALU.subtract, ALU.add, [Bg, Bc], [Bpg, Bf[E1]])
                    self.ACT(gwf(EGC), pg[:, 0:128], AF.Exp, [], [Bpg, Bf[EGC]])
                    yield
                    self.ACT(gwf(E2), gwf(E2), AF.Exp, [], [Bf[E2]], scale=-1.0)
                    self.ACT(gwf(E1), gwf(E1), AF.Exp, [], [Bf[E1]])
                    yield
                    A0F, PF_A, PF_B, PTF_A, PTF_B, TTF_A, TTF_B = range(5, 12)
                    self.STT(rr(gwf(A0F)), kk_sb, bet, gwf(E2), ALU.mult, ALU.mult, [Bkq, Bg, Bf[E2]], [Bf[A0F]])
                    self.TT("dve", gwb(ATT), qk_sb, gwf(E1), ALU.mult, [Bkq, Bf[E1]], [Bb[ATT]])
                    self.TT("dve", gwb(QG), qT[:, c0:c0 + 128], gwf(EGC), ALU.mult, [BW[QK], Bf[EGC]], [Bb[QG]])
                    yield
                    pa, Bpa = self.pm()
                    self.TR(pa[:, 0:128], gwf(A0F), self.ident_f, [Bf[A0F], Bc], [Bpa])
                    yield
                    self.STT(rr(gwf(TTF_A)), pa[:, 0:128], -1.0, self.ident_f, ALU.mult, ALU.add, [Bc], [Bpa, Bf[TTF_A]])
                    self.CP("act", rr(gwf(PTF_A)), pa[:, 0:128], [], [Bpa, Bf[PTF_A]])
                    yield
                    P, Pt, Tt = A0F, PTF_A, TTF_A
                    for lev in range(1, NLEV + 1):
                        Pn = PF_A if P != PF_A else PF_B
                        Ptn = PTF_A if Pt != PTF_A else PTF_B
                        Ttn = TTF_A if Tt != TTF_A else TTF_B
                        p1, Bp1 = self.pm()
                        self.MM(p1[:, 0:128], rr(gwf(Pt)), rr(gwf(P)), True, True, [Bf[Pt], Bf[P]], [Bp1])
                        yield
                        self.CP("act", rr(gwf(Pn)), p1[:, 0:128], [], [Bp1, Bf[Pn]])
                        yield
                        p2, Bp2 = self.pm()
                        if lev < NLEV:
                            self.TR(p2[:, 128:256], gwf(Pn), self.ident_f, [Bf[Pn], Bc], [Bp2])
                        self.MM(p2[:, 0:128], rr(gwf(Pn)), rr(gwf(Tt)), True, True, [Bf[Pn], Bf[Tt]], [Bp2])
                        yield
                        if lev < NLEV:
                            self.CP("act", rr(gwf(Ptn)), p2[:, 128:256], [], [Bp2, Bf[Ptn]])
                        if lev < NLEV:
                            self.TT("dve", rr(gwf(Ttn)), p2[:, 0:128], gwf(Tt), ALU.add, [Bf[Tt]], [Bp2, Bf[Ttn]])
                        else:
                            self.TT("dve", gwb(TT_A), p2[:, 0:128], gwf(Tt), ALU.add, [Bf[Tt]], [Bp2, Bb[TT_A]])
                        yield
                        P, Pt, Tt = Pn, Ptn, Ttn
                    Tt = TT_A
                    pu, Bpu = self.pm()
                    self.MM(pu[:, 0:128], gwb(Tt), gwb(VB), True, True, [Bb[Tt], Bb[VB]], [Bpu])
                    self.MM(pu[:, 128:256], gwb(KBG), gwb(Tt), True, True, [Bb[Tt], Bb[KBG]], [Bpu])
                    yield
                    self.CP("act", gwf(USB), pu[:, 0:128], [], [Bpu, Bf[USB]])
                    self.CP("dve", gwb(WT), pu[:, 128:256], [], [Bpu, Bb[WT]])
                    yield
                    for c in range(2):
                        R = slice(64 * c, 64 * c + 64)
                        pw, Bpw = self.pm()
                        self.MM(pw[:, 0:128], gwb(WT), gwb(cur), True, True, [Bb[WT], Bb[cur]], [Bpw])
                        yield
                        self.TT("dve", gw_b[R, VN, :], gw_f[R, USB, :], pw[R, 0:128], ALU.subtract, [Bf[USB]], [Bpw, Bb[VN]])
                        yield
                        oc = po[:, c0 + 64 * c:c0 + 64 * c + 64]
                        self.MM(pw[:, 128:256], gw_b[R, KD, :], gw_b[R, VN, :], True, True, [Bb[KD], Bb[VN]], [Bpw])
                        self.MM(oc, gwb(cur), gw_b[:, QG, 64 * c:64 * c + 64], True, False, [Bb[cur], Bb[QG]], [Bpo])
                        self.MM(oc, gw_b[R, VN, :], gw_b[R, ATT, 64 * c:64 * c + 64], False, True, [Bb[VN], Bb[ATT]], [Bpo])
                        yield
                        self.STT(S_f, S_f, gw_f[:, EGC, 64 * c + 63:64 * c + 64], pw[:, 128:256], ALU.mult, ALU.add,
                                 [Bf[EGC]], [Bpw, BS])
                        yield
                        cur = SB1 if cur == SB0 else SB0
                        self.CP("act", gwb(cur), S_f, [BS], [Bb[cur]])
                        yield
                self.ACT(osq_h, po[:], AF.Square, [], [Bpo, BW[OSQ_h]])
                self.CP("dve", W_[:, OSB_h, :], po[:], [], [Bpo, BW[OSB_h]])
                yield
                pt, Bpt = self.pm()
                self.MM(pt[:], self.ones_b[:], osq_h, True, True, [Bc, BW[OSQ_h]], [Bpt])
                yield
                self.ACT(W_[:, RS_h, :], pt[:], AF.Sqrt, [], [Bpt, BW[RS_h]], bias=EPS)
                yield
                rs_ap = W_[:, RS_h, :]
                self.sc.op("dve", lambda e, rs_ap=rs_ap: e.reciprocal(out=rs_ap, in_=rs_ap), w=[BW[RS_h]])
                yield
                self.STT(W_[:, OSB_h, :], W_[:, OSB_h, :], self.g_nw[:, 0:1], W_[:, RS_h, :], ALU.mult, ALU.mult, [Bp, BW[RS_h]], [BW[OSB_h]])
                yield
                self.TT("dve", self.hid[:, vh, :], W_[:, OSB_h, :], zs_h, ALU.mult, [BW[OSB_h], BZ], [self.B_hid[vh]])

            gens = [head_chain(0), head_chain(1)]
            while gens:
                for gen in list(gens):
                    try:
                        next(gen)
                    except StopIteration:
                        gens.remove(gen)


def _units(W, cols):
    Wr = W.reshape(KC, 128, W.shape[1])
    out = np.empty((len(cols), 128, KC, 128), np.float32)
    for u, c0 in enumerate(cols):
        out[u] = Wr[:, :, c0:c0 + 128].transpose(1, 0, 2)
    return out.reshape(len(cols), 128, KC * 128)


def _panels(W):
    K = W.shape[0]
    return np.ascontiguousarray(W.reshape(K // 128, 128, 4, 512).transpose(2, 1, 0, 3)).reshape(4, 128, (K // 128) * 512)


def _fm(v):
    return np.ascontiguousarray(v.reshape(-1, 128).T)


def layout_weights(layer_ids, ada_w, ada_b, norm_w, hg_w_in, hg_lb_logits, hg_norm_w, hg_w_out,
                   gdn_w_in, gdn_conv_w, gdn_A_log, gdn_dt_bias, gdn_norm_w, gdn_w_out,
                   ffn_w_gate_up, ffn_w_down):
    m = {}
    for li in layer_ids:
        j = li // 2
        m["ada_w_%d" % li] = np.ascontiguousarray(ada_w[li])
        m["ada_bT_%d" % li] = _fm(ada_b[li])
        m["normT_%d" % li] = np.concatenate([_fm(norm_w[li, i]) for i in range(4)], axis=1)
        cols = []
        for mm in range(FC):
            cols += [mm * 128, FFN_H + mm * 128]
        m["ffn_gu_%d" % li] = _units(ffn_w_gate_up[li], cols)
        m["ffn_dn_%d" % li] = _panels(ffn_w_down[li])
        if li % 2 == 0:
            cols = []
            for h in range(16):
                cols += [h * 128, 2048 + h * 128, 4096 + h * 128, 6144 + h * 128]
            m["hg_in_%d" % li] = _units(hg_w_in[j], cols)
            m["hg_out_%d" % li] = _panels(hg_w_out[j])
            m["hg_lbT_%d" % li] = np.concatenate([_fm(hg_lb_logits[0]), _fm(hg_lb_logits[j])], axis=1)
            m["hg_nw_%d" % li] = np.ascontiguousarray(hg_norm_w[j].reshape(128, 1))
        else:
            cols = []
            for g in range(16):
                cols += [g * 128, 2048 + g * 128, 4096 + (2 * g) * 128, 8192 + (2 * g) * 128,
                         4096 + (2 * g + 1) * 128, 8192 + (2 * g + 1) * 128]
            m["gdn_in_%d" % li] = _units(gdn_w_in[j], cols)
            ba = gdn_w_in[j][:, 12288:12352].reshape(KC, 128, 64).transpose(1, 0, 2)
            m["gdn_ba_%d" % li] = np.ascontiguousarray(ba).reshape(128, KC * 64)
            m["gdn_out_%d" % li] = _panels(gdn_w_out[j])
            cw = gdn_conv_w[j]
            m["gdn_convT_%d" % li] = np.ascontiguousarray(cw.reshape(4, 64, 128).transpose(2, 1, 0)).reshape(128, 256)
            al = np.concatenate([gdn_A_log[j], gdn_dt_bias[j]])[None, :]
            m["gdn_aldt_%d" % li] = np.ascontiguousarray(np.broadcast_to(al, (128, 64)))
            m["gdn_nw_%d" % li] = np.ascontiguousarray(gdn_norm_w[j].reshape(128, 1))
    return {k: np.ascontiguousarray(v, dtype=np.float32) for k, v in m.items()}


def host_consts():
    c = np.zeros((128, CST_COLS), np.float32)
    c[:, 0:128] = np.eye(128, dtype=np.float32)
    p = np.arange(128)
    c[:, 128:256] = ((p[:, None] // 32 == p[None, :] // 32) & (p[:, None] <= p[None, :])).astype(np.float32)
    r = np.ones(512, np.float32); r[0::32] = 0.0
    c[:, 256:768] = r[None, :]
    vm = np.zeros((128, 4, 128), np.float32)
    for n in range(4):
        vm[32 * n:32 * n + 32, n, :] = 1.0
    c[:, 768:1280] = vm.reshape(128, 512)
    same = (p[:, None] // 64 == p[None, :] // 64)
    c[:, 1280:1408] = (same & (p[:, None] <= p[None, :])).astype(np.float32)
    c[:, 1408:1536] = same.astype(np.float32)
    c[:, 1536:1664] = np.where(same & (p[:, None] <= p[None, :]), 0.0, -1.0e5)
    c[:, 1664:1792] = np.where(same & (p[None, :] < p[:, None]), 0.0, 1.0e5)
    c[:, 1792:1920] = 1.0
    return c


_PROG_CACHE = {}


def kernel(x, c, ada_w, ada_b, norm_w, hg_w_in, hg_lb_logits, hg_norm_w, hg_w_out,
           gdn_w_in, gdn_conv_w, gdn_A_log, gdn_dt_bias, gdn_norm_w, gdn_w_out,
           ffn_w_gate_up, ffn_w_down):
    x = np.asarray(x, np.float32)
    c = np.asarray(c, np.float32)
    Bt, S, _ = x.shape
    nseq = Bt // N_CORES
    layer_ids = list(range(DEPTH))
    key = (S, nseq, tuple(layer_ids))
    if key not in _PROG_CACHE:
        _PROG_CACHE[key] = Prog(S, nseq, layer_ids)
    prog = _PROG_CACHE[key]
    wm = layout_weights(layer_ids, *[np.asarray(a, np.float32) for a in (
        ada_w, ada_b, norm_w, hg_w_in, hg_lb_logits, hg_norm_w, hg_w_out, gdn_w_in, gdn_conv_w,
        gdn_A_log, gdn_dt_bias, gdn_norm_w, gdn_w_out, ffn_w_gate_up, ffn_w_down)])
    in_maps = []
    for core in range(N_CORES):
        xs = x[core * nseq:(core + 1) * nseq].reshape(nseq * S, D)
        cs = c[core * nseq:(core + 1) * nseq]
        cT = np.ascontiguousarray(cs.reshape(nseq, KC, 128).transpose(2, 1, 0))
        mp = {"x": np.ascontiguousarray(xs), "cT": cT, "cst": host_consts()}
        mp.update(wm)
        in_maps.append(mp)
    res = run_bass_kernel_spmd(prog.nc, in_maps, core_ids=list(range(N_CORES)))
    outs = [np.asarray(r["out"]).reshape(nseq, S, D) for r in res.results]
    return np.concatenate(outs, axis=0).astype(np.float32)
```
